# Optimizing a Trainium2 kernel written in Bass

```python
import jax, jax.numpy as jnp
from jax import lax
import numpy as np

D_MODEL = 1024
BATCH = 4
SEQ = 4096
DEPTH = 1

N_Q_HEADS = 16
N_KV_HEADS = 4
HEAD_DIM = 64
Q_WIDTH = N_Q_HEADS * HEAD_DIM
KV_WIDTH = N_KV_HEADS * HEAD_DIM
WINDOW = 128
BLOCK = 128
GMLP_WIDTH = 1024
GMLP_GROUPS = 8
GMLP_GROUP_DIM = GMLP_WIDTH // GMLP_GROUPS
CHUNK = 128
D_FF = 4 * D_MODEL
LN_EPS = 1e-5
DEEPNORM_ALPHA = (2 * DEPTH) ** 0.25
DEEPNORM_BETA = (8 * DEPTH) ** -0.25
IN_SPLITS = (Q_WIDTH, KV_WIDTH, KV_WIDTH, 2 * GMLP_WIDTH, D_MODEL, D_MODEL)
IN_WIDTH = sum(IN_SPLITS)

kernel_name = "hybrid_swa_gmlp_gated_deepnorm"


def layer_norm(x, g, b):
    xf = x.astype(jnp.float32)
    mu = jnp.mean(xf, axis=-1, keepdims=True)
    var = jnp.mean(jnp.square(xf - mu), axis=-1, keepdims=True)
    y = (xf - mu) * lax.rsqrt(var + LN_EPS)
    return (y * g.astype(jnp.float32) + b.astype(jnp.float32)).astype(x.dtype)


def sliding_window_attention(q, k, v, sinks):
    B, S = q.shape[0], q.shape[1]
    nb = S // BLOCK
    grp = N_Q_HEADS // N_KV_HEADS
    qb = q.reshape(B, nb, BLOCK, N_KV_HEADS, grp, HEAD_DIM)
    kb = k.reshape(B, nb, BLOCK, N_KV_HEADS, HEAD_DIM)
    vb = v.reshape(B, nb, BLOCK, N_KV_HEADS, HEAD_DIM)
    pad = ((0, 0), (1, 0), (0, 0), (0, 0), (0, 0))
    k_band = jnp.concatenate([jnp.pad(kb, pad)[:, :-1], kb], axis=2)
    v_band = jnp.concatenate([jnp.pad(vb, pad)[:, :-1], vb], axis=2)
    scale = HEAD_DIM ** -0.5
    scores = jnp.einsum('bnqhgd,bnshd->bnhgqs', qb, k_band).astype(jnp.float32) * scale
    q_pos = jnp.arange(BLOCK)[:, None] + BLOCK
    k_pos = jnp.arange(2 * BLOCK)[None, :]
    diff = q_pos - k_pos
    band = (diff >= 0) & (diff < WINDOW)
    blk = jnp.arange(nb)[:, None, None]
    valid = (blk > 0) | (k_pos[None] >= BLOCK)
    mask = band[None] & valid
    scores = jnp.where(mask[None, :, None, None], scores, -jnp.inf)
    sink = sinks.astype(jnp.float32).reshape(N_KV_HEADS, grp)[None, None, :, :, None, None]
    m = jnp.maximum(jnp.max(scores, axis=-1, keepdims=True), sink)
    p = jnp.exp(scores - m)
    denom = jnp.sum(p, axis=-1, keepdims=True) + jnp.exp(sink - m)
    probs = (p / denom).astype(v.dtype)
    out = jnp.einsum('bnhgqs,bnshd->bnqhgd', probs, v_band)
    return out.reshape(B, S, Q_WIDTH)


def chunked_spatial_gating(z, ln_g, ln_b, w_s, b_s):
    B, S = z.shape[0], z.shape[1]
    nc = S // CHUNK
    u, v = jnp.split(z, 2, axis=-1)
    v = layer_norm(v, ln_g, ln_b)
    vc = v.reshape(B, nc, CHUNK, GMLP_GROUPS, GMLP_GROUP_DIM)
    causal = jnp.tril(jnp.ones((CHUNK, CHUNK), dtype=bool))
    w = jnp.where(causal[None], w_s, jnp.zeros((), w_s.dtype))
    mixed = jnp.einsum('gts,bnsgd->bntgd', w, vc) + b_s.T[None, None, :, :, None]
    return u * mixed.reshape(B, S, GMLP_WIDTH)


def token_mixer(x, w_in, b_in, sinks, g_ln_g, g_ln_b, g_w_s, g_b_s, w_br_a, w_br_g, w_out):
    B, S = x.shape[0], x.shape[1]
    h = jnp.einsum('bsd,de->bse', x, w_in) + b_in
    offs = np.cumsum(IN_SPLITS)[:-1].tolist()
    q, k, v, z, gate_a, gate_g = jnp.split(h, offs, axis=-1)
    q = q.reshape(B, S, N_Q_HEADS, HEAD_DIM)
    k = k.reshape(B, S, N_KV_HEADS, HEAD_DIM)
    v = v.reshape(B, S, N_KV_HEADS, HEAD_DIM)
    y_a = sliding_window_attention(q, k, v, sinks) @ w_br_a
    y_g = chunked_spatial_gating(jax.nn.gelu(z), g_ln_g, g_ln_b, g_w_s, g_b_s) @ w_br_g
    mix = jax.nn.sigmoid(gate_a) * y_a + jax.nn.sigmoid(gate_g) * y_g
    return mix @ w_out


def squared_relu_mlp(x, w_up, w_down):
    return jnp.square(jax.nn.relu(x @ w_up)) @ w_down


def setup_inputs(seed: int = 0) -> dict:
    key = jax.random.key(seed)
    ks = jax.random.split(key, 18)
    f32 = jnp.float32
    L = DEPTH
    def nrm(k, shape, s):
        return jax.random.normal(k, shape, f32) * s
    return {
        "x": jax.random.normal(ks[0], (BATCH, SEQ, D_MODEL), f32),
        "w_in": nrm(ks[1], (L, D_MODEL, IN_WIDTH), D_MODEL ** -0.5),
        "b_in": nrm(ks[2], (L, IN_WIDTH), 0.02),
        "attn_sinks": nrm(ks[3], (L, N_Q_HEADS), 0.5),
        "gmlp_ln_g": 1.0 + nrm(ks[4], (L, GMLP_WIDTH), 0.05),
        "gmlp_ln_b": nrm(ks[5], (L, GMLP_WIDTH), 0.02),
        "gmlp_w_s": nrm(ks[6], (L, GMLP_GROUPS, CHUNK, CHUNK), CHUNK ** -0.5),
        "gmlp_b_s": 1.0 + nrm(ks[7], (L, GMLP_GROUPS, CHUNK), 0.1),
        "w_branch_attn": nrm(ks[8], (L, Q_WIDTH, D_MODEL), Q_WIDTH ** -0.5),
        "w_branch_gmlp": nrm(ks[9], (L, GMLP_WIDTH, D_MODEL), GMLP_WIDTH ** -0.5),
        "w_out": nrm(ks[10], (L, D_MODEL, D_MODEL), DEEPNORM_BETA * D_MODEL ** -0.5),
        "ln1_g": 1.0 + nrm(ks[11], (L, D_MODEL), 0.05),
        "ln1_b": nrm(ks[12], (L, D_MODEL), 0.02),
        "w_up": nrm(ks[13], (L, D_MODEL, D_FF), D_MODEL ** -0.5),
        "w_down": nrm(ks[14], (L, D_FF, D_MODEL), DEEPNORM_BETA * D_FF ** -0.5),
        "ln2_g": 1.0 + nrm(ks[15], (L, D_MODEL), 0.05),
        "ln2_b": nrm(ks[16], (L, D_MODEL), 0.02),
    }


def reference(x, w_in, b_in, attn_sinks, gmlp_ln_g, gmlp_ln_b, gmlp_w_s, gmlp_b_s,
              w_branch_attn, w_branch_gmlp, w_out, ln1_g, ln1_b, w_up, w_down, ln2_g, ln2_b):
    for l in range(DEPTH):
        mixed = token_mixer(x, w_in[l], b_in[l], attn_sinks[l], gmlp_ln_g[l], gmlp_ln_b[l],
                            gmlp_w_s[l], gmlp_b_s[l], w_branch_attn[l], w_branch_gmlp[l], w_out[l])
        x = layer_norm(DEEPNORM_ALPHA * x + mixed, ln1_g[l], ln1_b[l])
        x = layer_norm(DEEPNORM_ALPHA * x + squared_relu_mlp(x, w_up[l], w_down[l]), ln2_g[l], ln2_b[l])
    return x
```

```python
import contextlib
import numpy as np
import concourse.bass as bass
import concourse.mybir as mybir
from concourse.bass_utils import run_bass_kernel_spmd

F32 = mybir.dt.float32
BF16 = mybir.dt.bfloat16
AF = mybir.ActivationFunctionType
ALU = mybir.AluOpType
AX = mybir.AxisListType

D = 1024
SEQ = 4096
BATCH = 4
NCORES = 8
TOK = 2048
NB = 8
G = NB * 128
NG = TOK // G
NSLOT = 6
NSCR = 6
ENG_LN_G = "dve"
ENG_LN_B = "dve"
ENG_SG = "dve"
ALPHA = 2.0 ** 0.25
EPS = 1e-5
IN_W = 5632
DFF = 4096


class Buf:
    __slots__ = ("name", "last_w", "readers")

    def __init__(self, name):
        self.name = name
        self.last_w = None
        self.readers = []


class Op:
    __slots__ = ("eng", "fn", "deps", "dma", "chan", "ndma", "signal", "ev_sem", "ev_val")

    def __init__(self, eng, fn, dma, chan, ndma):
        self.eng = eng
        self.fn = fn
        self.deps = set()
        self.dma = dma
        self.chan = chan
        self.ndma = ndma
        self.signal = False
        self.ev_sem = None
        self.ev_val = None


ENGS = ("pe", "act", "dve", "pool", "sp")


class Prog:
    def __init__(self, nc, same_engine_sync=True):
        self.nc = nc
        self.ops = {e: [] for e in ENGS}
        self.same_engine_sync = same_engine_sync

    def op(self, eng, fn, reads=(), writes=(), dma=False, chan=None, ndma=1):
        o = Op(eng, fn, dma, chan, ndma)
        for b in reads:
            if b.last_w is not None:
                o.deps.add(b.last_w)
        for b in writes:
            if b.last_w is not None:
                o.deps.add(b.last_w)
            for r in b.readers:
                o.deps.add(r)
        for b in reads:
            b.readers.append(o)
        for b in writes:
            b.last_w = o
            b.readers = []
        o.deps.discard(o)
        if dma:
            if chan is None:
                chan = (list(writes) + list(reads))[0]
            o.chan = chan
        self.ops[eng].append(o)
        return o

    @staticmethod
    def alias_after(new_bufs, old_bufs):
        acc = []
        for b in old_bufs:
            if b.last_w is not None:
                acc.append(b.last_w)
            acc += b.readers
        for nb in new_bufs:
            nb.readers = nb.readers + acc

    def build(self, final_waits=()):
        nc = self.nc
        for e in ENGS:
            for o in self.ops[e]:
                keep = set()
                for d in o.deps:
                    if d.dma:
                        keep.add(d)
                    elif d.eng == o.eng and not o.dma:
                        if o.eng == "pe":
                            continue
                        if self.same_engine_sync:
                            keep.add(d)
                    else:
                        keep.add(d)
                o.deps = keep
                for d in keep:
                    d.signal = True
        for o in final_waits:
            o.signal = True
        chans = []
        seen = set()
        for e in ENGS:
            for o in self.ops[e]:
                if o.dma and id(o.chan) not in seen:
                    seen.add(id(o.chan))
                    chans.append(o.chan)
        with contextlib.ExitStack() as st:
            esem = {e: st.enter_context(nc.semaphore("s_" + e)) for e in ENGS if e != "sp"}
            csem = {id(c): st.enter_context(nc.semaphore("c_%d" % i)) for i, c in enumerate(chans)}
            ccount = {id(c): 0 for c in chans}
            for e in ENGS:
                cnt = 0
                for o in self.ops[e]:
                    if o.dma:
                        ccount[id(o.chan)] += 16 * o.ndma
                        o.ev_sem = csem[id(o.chan)]
                        o.ev_val = ccount[id(o.chan)]
                    elif o.signal:
                        cnt += 1
                        o.ev_sem = esem[e]
                        o.ev_val = cnt
            block = st.enter_context(nc.Block())
            prog = self

            def run(eng_name, eng, tail=()):
                known = {}
                for o in prog.ops[eng_name]:
                    need = {}
                    for d in o.deps:
                        k = id(d.ev_sem)
                        if k not in need or need[k][1] < d.ev_val:
                            need[k] = (d.ev_sem, d.ev_val)
                    for k, (s, v) in need.items():
                        if known.get(k, 0) >= v:
                            continue
                        eng.wait_ge(s, v)
                        known[k] = v
                    r = o.fn(eng)
                    if o.dma:
                        rs = r if isinstance(r, (list, tuple)) else [r]
                        assert len(rs) == o.ndma
                        for ins in rs:
                            ins.then_inc(o.ev_sem, 16)
                    elif o.signal:
                        r.then_inc(o.ev_sem, 1)
                for o in tail:
                    eng.wait_ge(o.ev_sem, o.ev_val)

            @block.tensor
            def _(eng):
                run("pe", eng)

            @block.scalar
            def _(eng):
                run("act", eng)

            @block.vector
            def _(eng):
                run("dve", eng)

            @block.gpsimd
            def _(eng):
                run("pool", eng)

            @block.sync
            def _(eng):
                run("sp", eng, tail=final_waits)


class _Stop(Exception):
    pass


def build_program(stop=None):
    nc = bass.Bass("TRN2", target_bir_lowering=False)

    def din(name, shape):
        return nc.dram_tensor(name, list(shape), F32, kind="ExternalInput").ap()

    xT_d = din("xT", [D, TOK + 128])
    x_d = din("x", [TOK, D])
    w_in_d = din("w_in", [D, IN_W])
    w_bra_d = din("w_bra", [D, D])
    w_brg_d = din("w_brg", [D, D])
    w_out_d = din("w_out", [D, D])
    w_up_d = din("w_up", [D, DFF])
    w_down_d = din("w_down", [DFF, D])
    bcol_d = din("bcol", [128, 68])
    brow_d = din("brow", [128, 256])
    bzv_d = din("bzv", [1, 1024])
    lnrep_d = din("lnrep", [128, 6, 1024])
    sinks_d = din("sinks", [128, 16])
    wsT_d = din("wsT", [128, 8, 128])
    trilT_d = din("trilT", [128, 128])
    bs_d = din("bs", [1, 1024])
    amask_d = din("amask", [128, 512])
    amask0_d = din("amask0", [128, 512])
    ident_d = din("ident", [128, 128])
    y_d = nc.dram_tensor("y", [TOK, D], F32, kind="ExternalOutput").ap()

    st = contextlib.ExitStack()
    with st:
        def sb(name, shape, dt=F32):
            return st.enter_context(nc.sbuf_tensor(name, list(shape), dt))

        P = Prog(nc)

        xT = sb("xT_s", [128, 8, G + 128], BF16)
        regA = sb("regA", [128, 16, G], BF16)
        regB = sb("regB", [128, 8192], F32)
        regC = sb("regC", [128, 8, G], BF16)
        ring = [sb("ring%d" % i, [128, 4096], BF16) for i in range(NSLOT)]
        scr = [sb("scr%d" % i, [128, 1024], F32) for i in range(NSCR)]
        xres = [sb("xres%d" % i, [128, 1024], F32) for i in range(2)]
        ablk_t = [sb("ablk%d" % i, [128, 1024], BF16) for i in range(2)]
        lnbuf = sb("lnbuf", [128, 2, 1024], F32)
        bcol = sb("bcol_s", [128, 68])
        brow = sb("brow_s", [128, 256])
        sinks = sb("sinks_s", [128, 16])
        negsink = sb("negsink", [128, 16])
        negsk4 = sb("negsk4", [128, 4])
        wsTb = sb("wsTb", [128, 8, 128], BF16)
        trilT = sb("trilT_s", [128, 128])
        rows = sb("rows", [1, 2176], BF16)
        ones_row = rows[0:1, 2048:2176]
        ones_bf = sb("ones_bf", [128, 128], BF16)
        Cg = sb("Cg", [128, 8, 128], F32)
        mhalf = sb("mhalf", [128, 1])
        cm1 = sb("cm1", [128, 4])
        amask = sb("amask_s", [128, 512], BF16)
        amask0 = sb("amask0_s", [128, 512], BF16)
        identb = sb("identb", [128, 128], BF16)
        epsc = sb("epsc", [128, 1])
        NST = 10
        stat = [sb("stat%d" % i, [128, 32]) for i in range(NST)]
        ps = st.enter_context(nc.psum_tensor("ps", [128, 8, 512], F32))

        attnT = regA[:, 0:8, :]
        mixT = regA[:, 8:16, :]
        hT = regA
        regB16 = regB[:].bitcast(BF16)
        regB32 = regB[:]
        qT = regB16[:, 0:8 * G].rearrange("p (e t) -> p e t", e=8)
        kdT = regB16[:, 8 * G:8 * G + 4 * (G + 128)].rearrange("p (h t) -> p h t", h=4)
        VW = 66
        voff = 8 * G + 4 * (G + 128)
        Vaug = regB16[:, voff:voff + (NB + 1) * 4 * VW].rearrange("p (b h w) -> p b h w", b=NB + 1, h=4)
        uT = regB16[:, 0:8 * G].rearrange("p (e t) -> p e t", e=8)
        vln = regB16[:, 8 * G:16 * G].rearrange("p (b f) -> p b f", b=NB)
        x1 = regB32.rearrange("p (b f) -> p b f", b=NB)
        sgT = regC
        x1T = regC

        XS = 640
        B_xT = [Buf("xTa"), Buf("xTb")]
        B_ring = [Buf("ring%d" % i) for i in range(NSLOT)]
        B_scr = [Buf("scr%d" % i) for i in range(NSCR)]
        B_xres = [Buf("xres%d" % i) for i in range(2)]
        B_ablk = [Buf("ablk%d" % i) for i in range(2)]
        B_ln = Buf("lnbuf")
        B_stat = [Buf("stat%d" % i) for i in range(NST)]
        B_statB = [Buf("statB%d" % i) for i in range(NST)]
        B_statC = [Buf("statC%d" % i) for i in range(NST)]
        B_ps = [Buf("ps%d" % i) for i in range(8)]
        B_const = Buf("const")

        state = {"ps": 0, "scr": 0, "stat": 0, "xres": 0, "ablk": 0}

        def ps1():
            i = state["ps"]
            state["ps"] = (i + 1) % 8
            return i

        def ps2():
            i = state["ps"]
            if i % 2:
                i = (i + 1) % 8
            state["ps"] = (i + 2) % 8
            return i

        def nscr():
            i = state["scr"]
            state["scr"] = (i + 1) % NSCR
            return scr[i], B_scr[i]

        def nstat():
            i = state["stat"]
            state["stat"] = (i + 1) % NST
            return stat[i], B_stat[i]

        def nstat3():
            i = state["stat"]
            state["stat"] = (i + 1) % NST
            return stat[i], B_stat[i], B_statB[i], B_statC[i]

        def ld(eng, dst, src, bufs):
            return P.op(eng, lambda e: e.dma_start(out=dst, in_=src), writes=bufs, dma=True)

        B_c = {n: Buf(n) for n in ["bcol", "brow", "sinks", "negsink", "wsTb", "trilT", "bs_hi", "bs_lo",
                                   "ones", "amask", "amask0", "identb", "eps"]}
        ld("sp", bcol[:], bcol_d[:, :], [B_c["bcol"]])
        ld("sp", brow[:], brow_d[:, :], [B_c["brow"]])
        ld("sp", sinks[:], sinks_d[:, :], [B_c["sinks"]])
        ld("sp", trilT[:], trilT_d[:, :], [B_c["trilT"]])
        sm0, bsm0 = nscr()
        ld("sp", sm0[:, 0:512], amask_d[:, :], [bsm0])
        P.op("dve", lambda e: e.tensor_copy(out=amask[:], in_=sm0[:, 0:512]), reads=[bsm0], writes=[B_c["amask"]])
        sm1, bsm1 = nscr()
        ld("sp", sm1[:, 0:512], amask0_d[:, :], [bsm1])
        P.op("dve", lambda e: e.tensor_copy(out=amask0[:], in_=sm1[:, 0:512]), reads=[bsm1], writes=[B_c["amask0"]])
        s0, bs0 = nscr()
        ld("sp", s0[:, 0:128], ident_d[:, :], [bs0])
        P.op("dve", lambda e: e.tensor_copy(out=identb[:], in_=s0[:, 0:128]), reads=[bs0], writes=[B_c["identb"]])
        s1, bs1 = nscr()
        ld("sp", s1[:, :], wsT_d.rearrange("p g t -> p (g t)"), [bs1])
        P.op("dve", lambda e: e.tensor_tensor(out=wsTb[:], in0=s1[:].rearrange("p (g t) -> p g t", g=8),
                                              in1=trilT[:].unsqueeze(1).broadcast_to([128, 8, 128]), op=ALU.mult),
             reads=[bs1, B_c["trilT"]], writes=[B_c["wsTb"]])
        for n_ in ["rows", "ones_bf", "Cg", "mhalf"]:
            B_c[n_] = Buf(n_)
        P.op("dve", lambda e: e.memset(rows[0:1, 2048:2176], 1.0), writes=[B_c["ones"]])
        P.op("dve", lambda e: e.memset(ones_bf[:], 1.0), writes=[B_c["ones_bf"]])
        P.op("dve", lambda e: e.memset(epsc[:], EPS), writes=[B_c["eps"]])
        P.op("dve", lambda e: e.memset(mhalf[:], -0.5), writes=[B_c["mhalf"]])
        P.op("dve", lambda e: e.memset(cm1[:], -1.0), writes=[B_c["mhalf"]])
        s2, bs2 = nscr()
        ld("sp", s2[0:1, :], bzv_d[:, :], [bs2])
        s3, bs3 = nscr()
        P.op("dve", lambda e: e.tensor_copy(out=rows[0:1, 0:1024], in_=s2[0:1, :]), reads=[bs2], writes=[B_c["rows"]])
        P.op("dve", lambda e: e.tensor_tensor(out=s3[0:1, :], in0=s2[0:1, :], in1=rows[0:1, 0:1024], op=ALU.subtract),
             reads=[bs2, B_c["rows"]], writes=[bs3])
        P.op("dve", lambda e: e.tensor_copy(out=rows[0:1, 1024:2048], in_=s3[0:1, :]), reads=[bs3], writes=[B_c["rows"]])
        s4, bs4 = nscr()
        ld("sp", s4[0:1, :], bs_d[:, :], [bs4])
        s5, bs5 = nscr()
        s5b = s5[:].bitcast(BF16)
        s6, bs6 = nscr()
        P.op("dve", lambda e: e.tensor_copy(out=s5b[0:1, 0:1024], in_=s4[0:1, :]), reads=[bs4], writes=[bs5])
        P.op("dve", lambda e: e.tensor_tensor(out=s6[0:1, :], in0=s4[0:1, :], in1=s5b[0:1, 0:1024], op=ALU.subtract),
             reads=[bs4, bs5], writes=[bs6])
        P.op("dve", lambda e: e.tensor_copy(out=s5b[0:1, 1024:2048], in_=s6[0:1, :]), reads=[bs6, bs5], writes=[bs5])
        for gi in range(8):
            b = ps1()

            def fC(e, gi=gi, b=b):
                e.matmul(ps[:, b, 0:128], lhsT=ones_bf[:, :], rhs=wsTb[:, gi, :], start=True, stop=True)
                e.matmul(ps[:, b, 128:256], lhsT=ones_row, rhs=s5b[0:1, gi * 128:(gi + 1) * 128], start=True, stop=False)
                return e.matmul(ps[:, b, 128:256], lhsT=ones_row, rhs=s5b[0:1, 1024 + gi * 128:1024 + (gi + 1) * 128], start=False, stop=True)
            P.op("pe", fC, reads=[B_c["ones_bf"], B_c["wsTb"], B_c["ones"], bs5], writes=[B_ps[b]])
            P.op("dve", (lambda gi, b: lambda e: e.tensor_copy(out=Cg[:, gi, :], in_=ps[:, b, 128:256]))(gi, b), reads=[B_ps[b]], writes=[B_c["Cg"]])
            P.op("dve", (lambda gi, b: lambda e: e.scalar_tensor_tensor(out=Cg[:, gi, :], in0=ps[:, b, 0:128], scalar=bcol[:, 44 + gi:45 + gi],
                                                                       in1=Cg[:, gi, :], op0=ALU.mult, op1=ALU.add))(gi, b),
                 reads=[B_ps[b], B_c["bcol"], B_c["Cg"]], writes=[B_c["Cg"]])
        P.op("dve", lambda e: e.tensor_scalar(out=negsink[:], in0=sinks[:], scalar1=-1.0, scalar2=None, op0=ALU.mult),
             reads=[B_c["sinks"]], writes=[B_c["negsink"]])
        P.op("dve", lambda e: e.tensor_reduce(out=negsk4[:], in_=negsink[:].rearrange("p (h j) -> p h j", h=4), axis=AX.X, op=ALU.min),
             reads=[B_c["negsink"]], writes=[B_c["negsink"]])

        w_in_v = w_in_d.rearrange("(k p) n -> p k n", p=128)
        w_bra_v = w_bra_d.rearrange("(k p) n -> p k n", p=128)
        w_brg_v = w_brg_d.rearrange("(k p) n -> p k n", p=128)
        w_out_v = w_out_d.rearrange("(k p) n -> p k n", p=128)
        w_up_v = w_up_d.rearrange("(k p) n -> p k n", p=128)
        w_down_v = w_down_d.rearrange("(f p) n -> p f n", p=128)

        def col_chunk(view, c0, ncols=512):
            def f(e, slot):
                dst = slot[:, 0:8 * ncols].rearrange("p (k n) -> p k n", k=8)
                return [e.dma_start(out=dst, in_=view[:, :, c0:c0 + ncols])]
            return f, 1

        def pair_chunk(viewA, cA, viewB, cB):
            def f(e, slot):
                dA = slot[:, 0:2048].rearrange("p (k n) -> p k n", k=8)
                dB = slot[:, 2048:4096].rearrange("p (k n) -> p k n", k=8)
                return [e.dma_start(out=dA, in_=viewA[:, :, cA:cA + 256]), e.dma_start(out=dB, in_=viewB[:, :, cB:cB + 256])]
            return f, 2

        def kdup_chunk():
            def f(e, slot):
                dst = slot[:, :].rearrange("p (k h r d) -> p k h r d", k=8, h=4, r=2)
                src = w_in_v[:, :, 1024:1280].rearrange("p k (h d) -> p k h d", h=4)
                return [e.dma_start(out=dst[:, :, h, r, :], in_=src[:, :, h, :]) for h in range(4) for r in range(2)]
            return f, 8

        def down_chunk(f0):
            def f(e, slot):
                dst = slot[:, :].rearrange("p (f n) -> p f n", f=4)
                return [e.dma_start(out=dst, in_=w_down_v[:, f0:f0 + 4, :])]
            return f, 1

        stream = []
        for g in range(NG):
            stream += [((g, "q", i), col_chunk(w_in_v, i * 512)) for i in range(2)]
            stream += [((g, "k", 0), col_chunk(w_in_v, 1024, 256))]
            stream += [((g, "v", 0), col_chunk(w_in_v, 1280, 256))]
            stream += [((g, "zv", i), col_chunk(w_in_v, 2560 + i * 512)) for i in range(2)]
            stream += [((g, "zu", i), col_chunk(w_in_v, 1536 + i * 512)) for i in range(2)]
            for i in range(4):
                stream += [((g, "pab", i), pair_chunk(w_bra_v, i * 256, w_brg_v, i * 256)),
                           ((g, "pgg", i), pair_chunk(w_in_v, 3584 + i * 256, w_in_v, 4608 + i * 256))]
            stream += [((g, "o", i), col_chunk(w_out_v, i * 512)) for i in range(2)]
            for fh in range(2):
                stream += [((g, "up", fh * 4 + i), col_chunk(w_up_v, (fh * 4 + i) * 512)) for i in range(4)]
                stream += [((g, "dn", fh * 4 + i), down_chunk((fh * 4 + i) * 4)) for i in range(4)]
        rs = {"next": 0, "free": list(range(NSLOT)), "where": {}}

        def ring_pump(limit=None):
            n_ = 0
            while rs["free"] and rs["next"] < len(stream) and (limit is None or n_ < limit):
                n_ += 1
                cid, (loader, ndma) = stream[rs["next"]]
                rs["next"] += 1
                s = rs["free"].pop(0)
                rs["where"][cid] = s
                P.op("pool", (lambda loader, s: lambda e: loader(e, ring[s]))(loader, s),
                     writes=[B_ring[s]], dma=True, ndma=ndma)

        def acquire(cid):
            ring_pump()
            assert cid in rs["where"], cid
            s = rs["where"][cid]
            return ring[s], B_ring[s]

        def release(cid):
            s = rs["where"].pop(cid)
            rs["free"].append(s)
            ring_pump()

        xT_v = xT_d.rearrange("(k p) t -> p k t", p=128)

        def load_xT(g, part):
            lo, hi = (0, XS) if part == 0 else (XS, G + 128)
            P.op("pool", lambda e: [e.dma_start(out=xT[:, :, lo:hi], in_=xT_v[:, :, g * G + lo:g * G + hi])],
                 writes=[B_xT[part]], dma=True, ndma=1)

        def mm_group(insts_fn, reads, pbufs):
            return P.op("pe", insts_fn, reads=reads, writes=pbufs)

        def ln_stats(src_ap, src_buf):
            stt, bst = nstat()
            P.op("dve", lambda e: e.bn_stats(out=stt[:, 0:6], in_=src_ap[:, 0:512]), reads=[src_buf], writes=[bst])
            P.op("dve", lambda e: e.bn_stats(out=stt[:, 6:12], in_=src_ap[:, 512:1024]), reads=[src_buf], writes=[bst])
            P.op("dve", lambda e: e.bn_aggr(out=stt[:, 12:14], in_=stt[:, 0:12]), reads=[bst], writes=[bst])
            P.op("pool", lambda e: e.tensor_tensor(out=stt[:, 14:15], in0=stt[:, 13:14], in1=epsc[:, 0:1], op=ALU.add),
                 reads=[bst, B_c["eps"]], writes=[bst])
            P.op("pool", lambda e: e.tensor_tensor(out=stt[:, 15:16], in0=stt[:, 14:15], in1=mhalf[:, 0:1], op=ALU.pow),
                 reads=[bst, B_c["mhalf"]], writes=[bst])
            P.op("pool", lambda e: e.tensor_tensor(out=stt[:, 17:18], in0=stt[:, 12:13], in1=cm1[:, 0:1], op=ALU.mult),
                 reads=[bst, B_c["mhalf"]], writes=[bst])
            P.op("pool", lambda e: e.tensor_tensor(out=stt[:, 16:17], in0=stt[:, 17:18], in1=stt[:, 15:16], op=ALU.mult),
                 reads=[bst], writes=[bst])
            return stt, bst

        def ln_normalize(src_ap, src_buf, stt, bst):
            P.op("act", lambda e: e.activation(out=src_ap, in_=src_ap, func=AF.Identity, bias=stt[:, 16:17], scale=stt[:, 15:16]),
                 reads=[src_buf, bst], writes=[src_buf])

        def ln_affine(src_ap, src_buf, gamma_ap, beta_ap, dst_ap, dst_bufs):
            P.op(ENG_LN_G, lambda e: e.tensor_tensor(out=src_ap, in0=src_ap, in1=gamma_ap, op=ALU.mult),
                 reads=[src_buf, B_ln], writes=[src_buf])
            P.op(ENG_LN_B, lambda e: e.tensor_tensor(out=dst_ap, in0=src_ap, in1=beta_ap, op=ALU.add),
                 reads=[src_buf, B_ln], writes=dst_bufs)

        def ln_apply(src_ap, src_buf, stt, bst, gamma_ap, beta_ap, dst_ap, dst_bufs):
            P.op("act", lambda e: e.activation(out=src_ap, in_=src_ap, func=AF.Identity, bias=stt[:, 16:17], scale=stt[:, 15:16]),
                 reads=[src_buf, bst], writes=[src_buf])
            P.op(ENG_LN_G, lambda e: e.tensor_tensor(out=src_ap, in0=src_ap, in1=gamma_ap, op=ALU.mult),
                 reads=[src_buf, B_ln], writes=[src_buf])
            P.op(ENG_LN_B, lambda e: e.tensor_tensor(out=dst_ap, in0=src_ap, in1=beta_ap, op=ALU.add),
                 reads=[src_buf, B_ln], writes=dst_bufs)

        def layer_norm_rows(src_ap, src_buf, gamma_ap, beta_ap, dst_ap, dst_bufs):
            stt, bst = ln_stats(src_ap, src_buf)
            ln_apply(src_ap, src_buf, stt, bst, gamma_ap, beta_ap, dst_ap, dst_bufs)

        def run_pipeline(n, stages, skew, hook=None):
            for t in range(n + max(skew)):
                for fn, sk in zip(stages, skew):
                    u = t - sk
                    if 0 <= u < n:
                        fn(u)
                if hook is not None:
                    hook(t)

        def load_ln(idx):
            P.op("sp", lambda e: e.dma_start(out=lnbuf[:], in_=lnrep_d[:, 2 * idx:2 * idx + 2, :]), writes=[B_ln], dma=True)

        def proj_fm(cids, rhs_ap_fn, rhs_bufs, evac_fn, ncols_chunks=8, post=None):
            for e_ in range(ncols_chunks):
                slot, bslot = acquire(cids[e_ // 4])
                wv = slot[:, :].rearrange("p (k n) -> p k n", k=8)
                for th in range(2):
                    b = ps1()

                    def f(e, wv=wv, e_=e_, th=th, b=b):
                        r = None
                        for k in range(8):
                            r = e.matmul(ps[:, b, :], lhsT=wv[:, k, (e_ % 4) * 128:(e_ % 4 + 1) * 128],
                                         rhs=rhs_ap_fn(k, th), start=(k == 0), stop=(k == 7))
                        return r
                    mm_group(f, [bslot] + (rhs_bufs(th) if callable(rhs_bufs) else rhs_bufs), [B_ps[b]])
                    evac_fn(e_, th, b)
                if post is not None:
                    post(e_)
                if e_ % 4 == 3:
                    release(cids[e_ // 4])

        out_ops = []

        def chk(level):
            if stop is not None and stop == level:
                raise _Stop()

        load_xT(0, 0)
        ring_pump(1)
        load_xT(0, 1)
        ring_pump()
        try:
          for g in range(NG):
            chk(0)
            B_qT = [Buf("qT%d" % e) for e in range(8)]
            B_kdT = Buf("kdT")
            B_V = Buf("V")
            B_attnT = Buf("attnT")
            if g > 0:
                for e_ in range(8):
                    Prog.alias_after([B_qT[e_]], [prev_B_x1[e_ // 2]])
                Prog.alias_after([B_kdT], prev_B_x1[4:7])
                Prog.alias_after([B_V], prev_B_x1[6:8])
                Prog.alias_after([B_attnT], prev_B_hT)

            def evac_q(e_, th, b):
                P.op("act", lambda e: e.activation(out=qT[:, e_, th * 512:(th + 1) * 512], in_=ps[:, b, :], func=AF.Identity,
                                                   bias=bcol[:, e_:e_ + 1], scale=1.0),
                     reads=[B_ps[b], B_c["bcol"]], writes=[B_qT[e_]])
            proj_fm([(g, "q", 0), (g, "q", 1)], lambda k, th: xT[:, k, 128 + th * 512:128 + (th + 1) * 512], lambda th: [B_xT[th]], evac_q)

            slot, bslot = acquire((g, "k", 0))
            wk = slot[:, 0:2048].rearrange("p (k n) -> p k n", k=8)
            for c_ in range(2):
                for (t0, n) in ((0, 512), (512, 512), (1024, 128)):
                    b = ps1()

                    def f(e, c_=c_, t0=t0, n=n, b=b, wk=wk):
                        r = None
                        for k in range(8):
                            r = e.matmul(ps[:, b, 0:n], lhsT=wk[:, k, c_ * 128:(c_ + 1) * 128], rhs=xT[:, k, t0:t0 + n], start=(k == 0), stop=(k == 7))
                        return r
                    mm_group(f, [bslot] + ([B_xT[0]] if t0 + n <= XS else (B_xT if t0 < XS else [B_xT[1]])), [B_ps[b]])
                    P.op("act", (lambda c_, t0, n, b: lambda e: e.activation(out=kdT[0:64, 2 * c_, t0:t0 + n], in_=ps[0:64, b, 0:n], func=AF.Identity,
                                                                            bias=bcol[0:64, 8 + c_:9 + c_], scale=1.0))(c_, t0, n, b),
                         reads=[B_ps[b], B_c["bcol"]], writes=[B_kdT])
                    P.op("act", (lambda c_, t0, n, b: lambda e: e.activation(out=kdT[64:128, 2 * c_ + 1, t0:t0 + n], in_=ps[64:128, b, 0:n], func=AF.Identity,
                                                                            bias=bcol[64:128, 8 + c_:9 + c_], scale=1.0))(c_, t0, n, b),
                         reads=[B_ps[b], B_c["bcol"]], writes=[B_kdT])
            P.op("sp", lambda e: [e.dma_start(out=kdT[64:128, 0, :], in_=kdT[0:64, 0, :]),
                                  e.dma_start(out=kdT[0:64, 1, :], in_=kdT[64:128, 1, :]),
                                  e.dma_start(out=kdT[64:128, 2, :], in_=kdT[0:64, 2, :]),
                                  e.dma_start(out=kdT[0:64, 3, :], in_=kdT[64:128, 3, :])],
                 reads=[B_kdT], writes=[B_kdT], dma=True, ndma=4)
            release((g, "k", 0))

            slot, bslot = acquire((g, "v", 0))
            wvv = slot[:, 0:2048].rearrange("p (k n) -> p k n", k=8)
            P.op("dve", lambda e: e.memset(Vaug[:, :, :, 64:65], 1.0), writes=[B_V])
            for blk in range(NB + 1):
                b = ps1()

                def f(e, blk=blk, b=b, wvv=wvv):
                    r = None
                    for k in range(8):
                        r = e.matmul(ps[:, b, 0:256], lhsT=xT[:, k, blk * 128:(blk + 1) * 128], rhs=wvv[:, k, :], start=(k == 0), stop=(k == 7))
                    return r
                mm_group(f, [bslot, B_xT[0] if (blk + 1) * 128 <= XS else B_xT[1]], [B_ps[b]])
                P.op("dve", (lambda blk, b: lambda e: e.tensor_tensor(out=Vaug[:, blk, :, 0:64],
                                                                      in0=ps[:, b, 0:256].rearrange("p (h d) -> p h d", h=4),
                                                                      in1=brow[:, 0:256].rearrange("p (h d) -> p h d", h=4), op=ALU.add))(blk, b),
                     reads=[B_ps[b], B_c["brow"]], writes=[B_V])
            release((g, "v", 0))

            chk(1)
            units = [(blk, hk) for blk in range(NB) for hk in range(4)]
            ctx = [dict() for _ in units]
            ablk_of = {}

            def att_A(u):
                blk, hk = units[u]
                c = ctx[u]
                if hk == 0:
                    i = state["ablk"]
                    state["ablk"] = (i + 1) % 2
                    ablk_of[blk] = (ablk_t[i], B_ablk[i])
                b2 = 2 * (u % 2)

                mask_t = amask0 if (g == 0 and blk == 0) else amask
                mask_b = B_c["amask0"] if (g == 0 and blk == 0) else B_c["amask"]

                def f(e, blk=blk, hk=hk, b2=b2):
                    r = None
                    for cb in range(2):
                        for b_ in range(2):
                            e.matmul(ps[:, b2 + b_, cb * 256:(cb + 1) * 256], lhsT=identb[:], rhs=mask_t[:, 0:256], start=True, stop=False)
                        for b_ in range(2):
                            j = cb * 2 + b_
                            hq = hk * 4 + j
                            base = (hq % 2) * 64
                            r = e.matmul(ps[:, b2 + b_, cb * 256:(cb + 1) * 256],
                                         lhsT=qT[base:base + 64, hq // 2, blk * 128:(blk + 1) * 128],
                                         rhs=kdT[base:base + 64, hk, blk * 128:blk * 128 + 256], start=False, stop=True)
                    return r
                mm_group(f, [B_qT[2 * hk], B_qT[2 * hk + 1], B_kdT, mask_b, B_c["identb"]], [B_ps[b2], B_ps[b2 + 1]])
                stt, bst, bstB, bstC = nstat3()
                Sall = ps[:, b2:b2 + 2, :].rearrange("p b n -> p (b n)")
                P.op("dve", lambda e: e.tensor_reduce(out=stt[:, 0:1], in_=Sall, axis=AX.X, op=ALU.max),
                     reads=[B_ps[b2], B_ps[b2 + 1]], writes=[bst])
                P.op("dve", lambda e: e.scalar_tensor_tensor(out=stt[:, 1:2], in0=stt[:, 0:1], scalar=-0.125,
                                                             in1=negsk4[:, hk:hk + 1], op0=ALU.mult, op1=ALU.min),
                     reads=[bst, B_c["negsink"]], writes=[bst])
                praw, bpraw = nscr()
                praw16 = praw[:].bitcast(BF16)
                bpts = Buf("pts")
                P.op("act", lambda e: e.activation(out=praw16[:, 0:1024], in_=Sall, func=AF.Exp, bias=stt[:, 1:2], scale=0.125),
                     reads=[B_ps[b2], B_ps[b2 + 1], bst], writes=[bpraw])
                P.op("pool", lambda e: e.tensor_tensor(out=stt[:, 8:12], in0=negsink[:, hk * 4:hk * 4 + 4], in1=stt[:, 1:2].broadcast_to([128, 4]),
                                                       op=ALU.subtract),
                     reads=[bst, B_c["negsink"]], writes=[bstB])
                P.op("act", lambda e: e.activation(out=stt[:, 12:16], in_=stt[:, 8:12], func=AF.Exp, scale=-1.0), reads=[bstB], writes=[bstB])
                c.update(bstB=bstB, bstC=bstC, bpts=bpts)
                c.update(stt=stt, bst=bst, praw16=praw16, bpraw=bpraw)

            def att_B1(u):
                blk, hk = units[u]
                c = ctx[u]
                praw16, bpraw = c["praw16"], c["bpraw"]
                bt = 4 + (u % 2)
                PTp = ps[:, bt, :].bitcast(BF16).rearrange("p (j c q) -> p j c q", j=4, c=2)

                def ft(e):
                    r = None
                    for j in range(4):
                        for cc in range(2):
                            pj = (j % 2) * 2 + j // 2
                            r = e.transpose(out=PTp[:, j, cc, :], in_=praw16[:, pj * 256 + cc * 128:pj * 256 + (cc + 1) * 128], identity=identb[:])
                    return r
                mm_group(ft, [bpraw, B_c["identb"]], [B_ps[bt]])
                c.update(bt=bt, PTp=PTp)

            def att_B2(u):
                blk, hk = units[u]
                c = ctx[u]
                praw16, bpraw, bt, PTp = c["praw16"], c["bpraw"], c["bt"], c["PTp"]
                PTs = praw16[:, 1024:2048].rearrange("p (j c q) -> p j c q", j=4, c=2)
                c.update(PTs=PTs)
                P.op("act", lambda e: e.activation(out=praw16[:, 1024:2048], in_=ps[:, bt, :].bitcast(BF16), func=AF.Copy),
                     reads=[B_ps[bt], bpraw], writes=[c["bpts"]])

            def att_B3(u):
                blk, hk = units[u]
                c = ctx[u]
                bpraw, PTs = c["bpraw"], c["PTs"]
                bo = 6
                Ov = ps[:, bo, :].rearrange("p (j w) -> p j w", w=128)

                def fo(e):
                    r = None
                    for j in range(4):
                        for cc in range(2):
                            r = e.matmul(Ov[:, j, 0:65], lhsT=PTs[:, j, cc, :], rhs=Vaug[:, blk + cc, hk, 0:65], start=(cc == 0), stop=(cc == 1))
                    return r
                mm_group(fo, [bpraw, c["bpts"], B_V], [B_ps[bo]])
                c.update(bo=bo, Ov=Ov)

            def att_C1a(u):
                c = ctx[u]
                stt, bo, Ov = c["stt"], c["bo"], c["Ov"]
                bstB, bstC = c["bstB"], c["bstC"]
                P.op("dve", lambda e: e.tensor_tensor(out=stt[:, 16:20], in0=Ov[:, :, 64:65].rearrange("p j o -> p (j o)"),
                                                      in1=stt[:, 12:16], op=ALU.add),
                     reads=[B_ps[bo], bstB], writes=[bstC])
                P.op("dve", lambda e: e.reciprocal(out=stt[:, 20:24], in_=stt[:, 16:20]), reads=[bstC], writes=[bstC])

            def att_C(u):
                blk, hk = units[u]
                c = ctx[u]
                stt, bst, bo, Ov = c["stt"], c["bst"], c["bo"], c["Ov"]
                ablk16, bablk = ablk_of[blk]
                bstB, bstC = c["bstB"], c["bstC"]
                P.op("dve", lambda e: e.tensor_tensor(
                    out=ablk16[:, hk * 256:(hk + 1) * 256].rearrange("p (j d) -> p j d", j=4), in0=Ov[:, :, 0:64],
                    in1=stt[:, 20:24].unsqueeze(2).broadcast_to([128, 4, 64]), op=ALU.mult),
                    reads=[B_ps[bo], bstC], writes=[bablk])

            def att_C2(u):
                blk, hk = units[u]
                ablk16, bablk = ablk_of[blk]
                if hk == 3:
                    bt = 7
                    ATp = ps[:, bt, :].bitcast(BF16).rearrange("p (e q) -> p e q", e=8)

                    def fa(e):
                        r = None
                        for e_ in range(8):
                            r = e.transpose(out=ATp[:, e_, :], in_=ablk16[:, e_ * 128:(e_ + 1) * 128], identity=identb[:])
                        return r
                    mm_group(fa, [bablk, B_c["identb"]], [B_ps[bt]])
                    P.op("dve", lambda e: e.tensor_copy(out=attnT[:, :, blk * 128:(blk + 1) * 128], in_=ATp),
                         reads=[B_ps[bt]], writes=[B_attnT])

            run_pipeline(len(units), [att_B2, att_C1a, att_C, att_A, att_B1, att_B3, att_C2], [3, 5, 5, 0, 2, 4, 6])

            chk(2)
            B_uT = [Buf("uT%d" % e) for e in range(8)]
            B_vln = [Buf("vln%d" % b) for b in range(NB)]
            Prog.alias_after(B_uT + B_vln, B_qT + [B_kdT, B_V])
            B_sgT = [Buf("sgT%d" % q_) for q_ in range(NB // 4)]
            if g > 0:
                Prog.alias_after(B_sgT, prev_B_x1T)

            sl0, bsl0 = acquire((g, "zv", 0))
            sl1, bsl1 = acquire((g, "zv", 1))
            wz = [sl0[:, :].rearrange("p (k n) -> p k n", k=8), sl1[:, :].rearrange("p (k n) -> p k n", k=8)]
            zctx = [dict() for _ in range(NB)]

            def zv_A(blk):
                b2 = ps2()

                def f(e, wz=wz):
                    r = None
                    for half in range(2):
                        o = ps[:, b2 + half, :]
                        e.matmul(o, lhsT=ones_row, rhs=rows[0:1, half * 512:(half + 1) * 512], start=True, stop=False)
                        e.matmul(o, lhsT=ones_row, rhs=rows[0:1, 1024 + half * 512:1024 + (half + 1) * 512], start=False, stop=False)
                        for k in range(8):
                            r = e.matmul(o, lhsT=xT[:, k, 128 + blk * 128:128 + (blk + 1) * 128], rhs=wz[half][:, k, :],
                                         start=False, stop=(k == 7))
                    return r
                mm_group(f, [bsl0, bsl1, B_xT[0] if 128 + (blk + 1) * 128 <= XS else B_xT[1], B_c["rows"], B_c["ones"]], [B_ps[b2], B_ps[b2 + 1]])
                zt, bzt = nscr()
                P.op("act", lambda e: e.activation(out=zt[:, :], in_=ps[:, b2:b2 + 2, :].rearrange("p b n -> p (b n)"), func=AF.Gelu_apprx_tanh),
                     reads=[B_ps[b2], B_ps[b2 + 1]], writes=[bzt])
                stt, bst = ln_stats(zt[:, :], bzt)
                zctx[blk].update(zt=zt, bzt=bzt, stt=stt, bst=bst)

            def zv_B(blk):
                c = zctx[blk]
                zt, bzt, stt, bst = c["zt"], c["bzt"], c["stt"], c["bst"]
                P.op("act", lambda e: e.activation(out=vln[:, blk, :], in_=zt[:, :], func=AF.Identity, bias=stt[:, 16:17], scale=stt[:, 15:16]),
                     reads=[bzt, bst], writes=[B_vln[blk]])

            def spatial(gi, quad):
                b = ps1()

                def f(e):
                    r = None
                    for b4 in range(4):
                        blk = quad * 4 + b4
                        r = e.matmul(ps[:, b, b4 * 128:(b4 + 1) * 128], lhsT=vln[:, blk, gi * 128:(gi + 1) * 128], rhs=wsTb[:, gi, :],
                                     start=True, stop=True)
                    return r
                mm_group(f, [B_vln[quad * 4 + i] for i in range(4)] + [B_c["wsTb"]], [B_ps[b]])
                tq, btq = nscr()
                P.op("dve", lambda e: e.scalar_tensor_tensor(
                    out=tq[:, 0:512].rearrange("p (a t) -> p a t", a=4), in0=ps[:, b, :].rearrange("p (a t) -> p a t", a=4),
                    scalar=bcol[:, 36 + gi:37 + gi], in1=Cg[:, gi, :].unsqueeze(1).broadcast_to([128, 4, 128]),
                    op0=ALU.mult, op1=ALU.add),
                    reads=[B_ps[b], B_c["bcol"], B_c["Cg"]], writes=[btq])
                P.op(ENG_SG, lambda e: e.tensor_tensor(out=sgT[:, gi, quad * 512:(quad + 1) * 512], in0=tq[:, 0:512],
                                                       in1=uT[:, gi, quad * 512:(quad + 1) * 512], op=ALU.mult),
                     reads=[btq, B_uT[gi]], writes=[B_sgT[quad]])

            def evac_u(e_, th, b):
                P.op("act", lambda e: e.activation(out=uT[:, e_, th * 512:(th + 1) * 512], in_=ps[:, b, :], func=AF.Gelu_apprx_tanh,
                                                   bias=bcol[:, 12 + e_:13 + e_], scale=1.0),
                     reads=[B_ps[b], B_c["bcol"]], writes=[B_uT[e_]])
            run_pipeline(NB, [zv_A, zv_B], [0, 1])
            release((g, "zv", 0))
            release((g, "zv", 1))

            def post_u(e_):
                for quad in range(NB // 4):
                    spatial(e_, quad)
            proj_fm([(g, "zu", 0), (g, "zu", 1)], lambda k, th: xT[:, k, 128 + th * 512:128 + (th + 1) * 512], lambda th: [B_xT[th]], evac_u, post=post_u)

            chk(3)
            B_mixT = Buf("mixT")
            if g > 0:
                Prog.alias_after([B_mixT], prev_B_hT)
            for e_ in range(8):
                i = e_ // 2
                slAB = acquire((g, "pab", i))
                slGG = acquire((g, "pgg", i))
                sls = [slAB, slAB, slGG, slGG]
                wvs = [sl[0][:, hf * 2048:(hf + 1) * 2048].rearrange("p (k n) -> p k n", k=8) for sl, hf in zip(sls, (0, 1, 0, 1))]
                for th in range(2):
                    rhs_list = [(attnT, 0, [B_attnT]), (sgT, 0, [B_sgT[th]]), (xT, 128, [B_xT[th]]), (xT, 128, [B_xT[th]])]
                    bb = [ps1() for _ in range(4)]
                    for m in range(4):
                        src, off, rb = rhs_list[m]

                        def f(e, m=m, src=src, off=off, b=bb[m], wv=wvs[m], e_=e_, th=th):
                            r = None
                            for k in range(8):
                                r = e.matmul(ps[:, b, :], lhsT=wv[:, k, (e_ % 2) * 128:(e_ % 2 + 1) * 128],
                                             rhs=src[:, k, off + th * 512:off + (th + 1) * 512], start=(k == 0), stop=(k == 7))
                            return r
                        mm_group(f, [sls[m][1]] + rb, [B_ps[bb[m]]])
                    sg_, bsg_ = nscr()
                    P.op("act", (lambda sg_, b, e_: lambda e: e.activation(out=sg_[:, 0:512], in_=ps[:, b, :], func=AF.Sigmoid,
                                                                          bias=bcol[:, 20 + e_:21 + e_], scale=1.0))(sg_, bb[2], e_),
                         reads=[B_ps[bb[2]], B_c["bcol"]], writes=[bsg_])
                    P.op("act", (lambda sg_, b, e_: lambda e: e.activation(out=sg_[:, 512:1024], in_=ps[:, b, :], func=AF.Sigmoid,
                                                                          bias=bcol[:, 28 + e_:29 + e_], scale=1.0))(sg_, bb[3], e_),
                         reads=[B_ps[bb[3]], B_c["bcol"]], writes=[bsg_])
                    P.op("dve", (lambda sg_, b: lambda e: e.tensor_tensor(out=sg_[:, 0:512], in0=ps[:, b, :], in1=sg_[:, 0:512], op=ALU.mult))(sg_, bb[0]),
                         reads=[B_ps[bb[0]], bsg_], writes=[bsg_])
                    P.op("dve", (lambda sg_, b: lambda e: e.tensor_tensor(out=sg_[:, 512:1024], in0=ps[:, b, :], in1=sg_[:, 512:1024], op=ALU.mult))(sg_, bb[1]),
                         reads=[B_ps[bb[1]], bsg_], writes=[bsg_])
                    P.op("dve", (lambda sg_, e_, th: lambda e: e.tensor_tensor(out=mixT[:, e_, th * 512:(th + 1) * 512], in0=sg_[:, 0:512],
                                                                              in1=sg_[:, 512:1024], op=ALU.add))(sg_, e_, th),
                         reads=[bsg_], writes=[B_mixT])
                if e_ % 2 == 1:
                    release((g, "pab", i))
                    release((g, "pgg", i))
            if g + 1 < NG:
                load_xT(g + 1, 0)
                load_xT(g + 1, 1)

            chk(4)
            B_x1 = [Buf("x1_%d" % b) for b in range(NB)]
            Prog.alias_after(B_x1, B_uT + B_vln)
            B_x1T = [Buf("x1T%d" % e_) for e_ in range(8)]
            Prog.alias_after(B_x1T, B_sgT)
            load_ln(1)
            so0, bso0 = acquire((g, "o", 0))
            so1, bso1 = acquire((g, "o", 1))
            wo = [so0[:, :].rearrange("p (k n) -> p k n", k=8), so1[:, :].rearrange("p (k n) -> p k n", k=8)]
            octx = [dict() for _ in range(NB)]

            def o_A(blk):
                b2 = ps2()

                def f(e, wo=wo):
                    r = None
                    for half in range(2):
                        for k in range(8):
                            r = e.matmul(ps[:, b2 + half, :], lhsT=mixT[:, k, blk * 128:(blk + 1) * 128], rhs=wo[half][:, k, :],
                                         start=(k == 0), stop=(k == 7))
                    return r
                mm_group(f, [bso0, bso1, B_mixT], [B_ps[b2], B_ps[b2 + 1]])
                xi = state["xres"]
                state["xres"] = 1 - xi
                row0 = g * G + blk * 128
                P.op("sp", lambda e: e.dma_start(out=xres[xi][:], in_=x_d[row0:row0 + 128, :]), writes=[B_xres[xi]], dma=True)
                r1, br1 = scr[blk % 3], B_scr[blk % 3]
                P.op("dve", lambda e: e.scalar_tensor_tensor(out=r1[:, :], in0=xres[xi][:], scalar=ALPHA,
                                                             in1=ps[:, b2:b2 + 2, :].rearrange("p b n -> p (b n)"),
                                                             op0=ALU.mult, op1=ALU.add),
                     reads=[B_xres[xi], B_ps[b2], B_ps[b2 + 1]], writes=[br1])
                stt, bst = ln_stats(r1[:, :], br1)
                octx[blk].update(r1=r1, br1=br1, stt=stt, bst=bst)

            def o_B(blk):
                c = octx[blk]
                r1, br1, stt, bst = c["r1"], c["br1"], c["stt"], c["bst"]
                P.op("act", lambda e: e.activation(out=x1[:, blk, :], in_=r1[:, :], func=AF.Identity, bias=stt[:, 16:17], scale=stt[:, 15:16]),
                     reads=[br1, bst], writes=[B_x1[blk]])
                xi3 = blk % 4
                xb, bxb = (scr[3] if xi3 < 2 else scr[5]), B_xb[xi3]
                xb16 = xb[:].bitcast(BF16)[:, (xi3 % 2) * 1024:(xi3 % 2 + 1) * 1024]
                P.op("act", lambda e: e.activation(out=xb16, in_=r1[:, :], func=AF.Identity, bias=stt[:, 16:17], scale=stt[:, 15:16]),
                     reads=[br1, bst], writes=[bxb])
                c.update(xb16=xb16, bxb=bxb)

            def o_C(blk):
                if blk % 2 == 0:
                    return
                b0 = blk - 1
                cs = (octx[b0], octx[blk])
                bt0 = ps1()
                bt1 = ps1()
                XT0 = ps[:, bt0, :].bitcast(BF16).rearrange("p (e t) -> p e t", e=4)
                XT1 = ps[:, bt1, :].bitcast(BF16).rearrange("p (e t) -> p e t", e=4)

                def fx(e):
                    r = None
                    for bi in range(2):
                        xb16 = cs[bi]["xb16"]
                        for e_ in range(8):
                            dst = (XT0 if e_ % 2 == 0 else XT1)[:, e_ // 2, bi * 128:(bi + 1) * 128]
                            r = e.transpose(out=dst, in_=xb16[:, e_ * 128:(e_ + 1) * 128], identity=identb[:])
                    return r
                mm_group(fx, [cs[0]["bxb"], cs[1]["bxb"], B_c["identb"]], [B_ps[bt0], B_ps[bt1]])
                for e_ in range(8):
                    o_ap = x1T[:, e_, b0 * 128:(b0 + 2) * 128]
                    if e_ % 2 == 0:
                        P.op("act", (lambda e_, o_ap: lambda e: e.activation(out=o_ap, in_=XT0[:, e_ // 2, :], func=AF.Identity,
                                                                            bias=bcol[:, 60 + e_:61 + e_], scale=bcol[:, 52 + e_:53 + e_]))(e_, o_ap),
                             reads=[B_ps[bt0], B_c["bcol"]], writes=[B_x1T[e_]])
                    else:
                        P.op("dve", (lambda e_, o_ap: lambda e: e.tensor_scalar(out=o_ap, in0=XT1[:, e_ // 2, :], scalar1=bcol[:, 52 + e_:53 + e_],
                                                                               scalar2=bcol[:, 60 + e_:61 + e_], op0=ALU.mult, op1=ALU.add))(e_, o_ap),
                             reads=[B_ps[bt1], B_c["bcol"]], writes=[B_x1T[e_]])

            B_hT_lo = Buf("hT_lo")
            Prog.alias_after([B_hT_lo], [B_attnT])
            B_hT = [B_hT_lo, None]

            def up_unit(fh, fi, th, sq_half=None):
                fidx = fh * 16 + fi
                slot, bslot = acquire((g, "up", fidx // 4))
                wv = slot[:, :].rearrange("p (k n) -> p k n", k=8)
                b = ps1()

                def f(e):
                    r = None
                    for k in range(8):
                        r = e.matmul(ps[:, b, :], lhsT=wv[:, k, (fidx % 4) * 128:(fidx % 4 + 1) * 128],
                                     rhs=x1T[:, k, th * 512:(th + 1) * 512], start=(k == 0), stop=(k == 7))
                    return r
                mm_group(f, [bslot] + B_x1T, [B_ps[b]])
                if sq_half is None:
                    sq_t, bsq = nscr()
                    sq = sq_t[:, 0:512]
                else:
                    sq, bsq = scr[4][:, sq_half * 512:(sq_half + 1) * 512], B_scr[4]
                P.op("act", lambda e: e.activation(out=sq, in_=ps[:, b, :], func=AF.Square), reads=[B_ps[b]], writes=[bsq])
                P.op("dve", lambda e: e.scalar_tensor_tensor(out=hT[:, fi, th * 512:(th + 1) * 512], in0=ps[:, b, :],
                                                             scalar=0.0, in1=sq, op0=ALU.is_gt, op1=ALU.mult),
                     reads=[B_ps[b], bsq], writes=[B_hT[fi // 8]])


            def p6_hook(t):
                if t == 7:
                    B_hT_hi = Buf("hT_hi")
                    Prog.alias_after([B_hT_hi], [B_mixT])
                    B_hT[1] = B_hT_hi
                if 7 <= t <= 10:
                    for fi in (2 * (t - 7), 2 * (t - 7) + 1, 8 + 2 * (t - 7), 9 + 2 * (t - 7)):
                        up_unit(0, fi, 0, sq_half=fi % 2)

            B_xb = [Buf("xb%d" % i_) for i_ in range(4)]
            Prog.alias_after(B_xb[0:2], [B_scr[3]])
            Prog.alias_after(B_xb[2:4], [B_scr[5]])
            run_pipeline(NB, [o_A, o_B, o_C], [0, 1, 3], hook=p6_hook)
            Prog.alias_after([B_scr[3]], B_xb[0:2])
            Prog.alias_after([B_scr[5]], B_xb[2:4])
            release((g, "o", 0))
            release((g, "o", 1))

            chk(5)
            for fh in range(2):
                if fh == 0:
                    for fi in range(16):
                        up_unit(0, fi, 1)
                        if fi % 4 == 3:
                            release((g, "up", fi // 4))
                    rest = range(0)
                else:
                    rest = range(16)
                for fi in rest:
                    for th in range(2):
                        up_unit(fh, fi, th)
                    if fi % 4 == 3:
                        release((g, "up", (fh * 16 + fi) // 4))
                if fh == 1:
                    load_ln(2)
                dsl = [acquire((g, "dn", fh * 4 + i)) for i in range(4)]
                wd = [s[0][:, :].rearrange("p (f n) -> p f n", f=4) for s in dsl]
                dctx = [dict() for _ in range(NB)]

                def d_A(blk, fh=fh, wd=wd, dsl=dsl):
                    b2 = ps2()

                    def f(e):
                        r = None
                        for fi in range(16):
                            for half in range(2):
                                r = e.matmul(ps[:, b2 + half, :], lhsT=hT[:, fi, blk * 128:(blk + 1) * 128],
                                             rhs=wd[fi // 4][:, fi % 4, half * 512:(half + 1) * 512], start=(fi == 0), stop=(fi == 15))
                        return r
                    mm_group(f, [s_[1] for s_ in dsl] + B_hT, [B_ps[b2], B_ps[b2 + 1]])
                    pflat = ps[:, b2:b2 + 2, :].rearrange("p b n -> p (b n)")
                    if fh == 0:
                        P.op("dve", lambda e: e.tensor_tensor(out=x1[:, blk, :], in0=x1[:, blk, :], in1=lnbuf[:, 0, :], op=ALU.mult),
                             reads=[B_x1[blk], B_ln], writes=[B_x1[blk]])
                        P.op("dve", lambda e: e.tensor_tensor(out=x1[:, blk, :], in0=x1[:, blk, :], in1=lnbuf[:, 1, :], op=ALU.add),
                             reads=[B_x1[blk], B_ln], writes=[B_x1[blk]])
                        P.op("dve", lambda e: e.scalar_tensor_tensor(out=x1[:, blk, :], in0=x1[:, blk, :], scalar=ALPHA, in1=pflat,
                                                                     op0=ALU.mult, op1=ALU.add),
                             reads=[B_x1[blk], B_ps[b2], B_ps[b2 + 1]], writes=[B_x1[blk]])
                    else:
                        r2, br2 = scr[blk % 3], B_scr[blk % 3]
                        P.op("dve", lambda e: e.tensor_tensor(out=r2[:, :], in0=x1[:, blk, :], in1=pflat, op=ALU.add),
                             reads=[B_x1[blk], B_ps[b2], B_ps[b2 + 1]], writes=[br2])
                        stt, bst = ln_stats(r2[:, :], br2)
                        dctx[blk].update(r2=r2, br2=br2, stt=stt, bst=bst)

                def d_B(blk):
                    c = dctx[blk]
                    ln_normalize(c["r2"][:, :], c["br2"], c["stt"], c["bst"])

                def d_C(blk, g=g):
                    c = dctx[blk]
                    r2, br2 = c["r2"], c["br2"]
                    ln_affine(r2[:, :], br2, lnbuf[:, 0, :], lnbuf[:, 1, :], r2[:, :], [br2])
                    row0 = g * G + blk * 128
                    out_ops.append(P.op("sp", lambda e: e.dma_start(out=y_d[row0:row0 + 128, :], in_=r2[:, :]), reads=[br2], dma=True))

                if fh == 0:
                    for blk in range(NB):
                        d_A(blk)
                else:
                    run_pipeline(NB, [d_A, d_B, d_C], [0, 1, 2])
                for i in range(4):
                    release((g, "dn", fh * 4 + i))
            prev_B_x1 = B_x1
            prev_B_x1T = B_x1T
            prev_B_hT = B_hT

        except _Stop:
            dz, bdz = nscr()
            P.op("dve", lambda e: e.memset(dz[:, :], 0.0), writes=[bdz])
            out_ops.append(P.op("sp", lambda e: e.dma_start(out=y_d[0:128, :], in_=dz[:, :]), reads=[bdz], dma=True))
        P.build(final_waits=out_ops)
    return nc


def _rep(v, n=128):
    return np.ascontiguousarray(np.broadcast_to(np.asarray(v, np.float32)[None], (n,) + tuple(np.shape(v))))


STOP = None


def kernel(x, w_in, b_in, attn_sinks, gmlp_ln_g, gmlp_ln_b, gmlp_w_s, gmlp_b_s,
           w_branch_attn, w_branch_gmlp, w_out, ln1_g, ln1_b, w_up, w_down, ln2_g, ln2_b):
    f = lambda a: np.ascontiguousarray(np.asarray(a, dtype=np.float32))
    x = f(x)
    w_in0 = f(w_in)[0]
    b = f(b_in)[0]
    bq = b[0:1024].reshape(8, 128).T
    bk = b[1024:1280].reshape(4, 64)
    bkd = np.concatenate([b[1024:1280].reshape(2, 128).T, np.zeros((128, 2), np.float32)], axis=1)
    bzu = b[1536:2560].reshape(8, 128).T
    bga = b[3584:4608].reshape(8, 128).T
    bgg = b[4608:5632].reshape(8, 128).T
    gcol = f(gmlp_ln_g)[0].reshape(8, 128).T
    gbcol = f(gmlp_ln_b)[0].reshape(8, 128).T
    g1col = f(ln1_g)[0].reshape(8, 128).T
    b1col = f(ln1_b)[0].reshape(8, 128).T
    bcol = f(np.concatenate([bq, bkd, bzu, bga, bgg, gcol, gbcol, g1col, b1col], axis=1))
    brow = f(_rep(b[1280:1536]))
    bzv = f(b[2560:3584].reshape(1, 1024))
    lnrep = f(np.stack([_rep(f(gmlp_ln_g)[0]), _rep(f(gmlp_ln_b)[0]), _rep(f(ln1_g)[0]), _rep(f(ln1_b)[0]),
                        _rep(f(ln2_g)[0]), _rep(f(ln2_b)[0])], axis=1))
    sinks = _rep(f(attn_sinks)[0])
    wsT = f(np.transpose(f(gmlp_w_s)[0], (2, 0, 1)))
    si = np.arange(128)
    trilT = f((si[:, None] <= si[None, :]).astype(np.float32))
    bs = f(f(gmlp_b_s)[0].reshape(1, 1024))
    m_prev = (si[None, :] > si[:, None]).astype(np.float32)
    m_cur = (si[None, :] <= si[:, None]).astype(np.float32)
    NEG = -29952.0
    a1 = np.concatenate([m_prev, m_cur], axis=1)
    a0 = np.concatenate([np.zeros_like(m_prev), m_cur], axis=1)
    amask = f(np.tile((1.0 - a1) * NEG, (1, 2)))
    amask_first = f(np.tile((1.0 - a0) * NEG, (1, 2)))
    ident = f(np.eye(128, dtype=np.float32))
    shared = {
        "w_in": w_in0, "w_bra": f(w_branch_attn)[0], "w_brg": f(w_branch_gmlp)[0], "w_out": f(w_out)[0],
        "w_up": f(w_up)[0], "w_down": f(w_down)[0], "bcol": bcol, "brow": brow, "bzv": bzv, "lnrep": lnrep, "sinks": sinks,
        "wsT": wsT, "trilT": trilT, "bs": bs, "amask": amask, "ident": ident,
    }
    in_maps = []
    for c in range(NCORES):
        bi, half = c // 2, c % 2
        t0 = half * TOK
        xs = x[bi, t0:t0 + TOK]
        if half == 0:
            halo = np.zeros((128, D), np.float32)
            am0 = amask_first
        else:
            halo = x[bi, t0 - 128:t0]
            am0 = amask
        xT = f(np.concatenate([halo, xs], axis=0).T)
        m = dict(shared)
        m["xT"] = xT
        m["x"] = f(xs)
        m["amask0"] = am0
        in_maps.append(m)
    nc = build_program(STOP)
    res = run_bass_kernel_spmd(nc, in_maps, core_ids=list(range(NCORES)))
    out = np.empty((BATCH, SEQ, D), np.float32)
    for c in range(NCORES):
        bi, half = c // 2, c % 2
        out[bi, half * TOK:(half + 1) * TOK] = res.results[c]["y"]
    return out
```

```python
import contextlib
import numpy as np
import concourse.bass as bass
import concourse.mybir as mybir
from concourse.bass_utils import run_bass_kernel_spmd

F32 = mybir.dt.float32
BF16 = mybir.dt.bfloat16
AF = mybir.ActivationFunctionType
ALU = mybir.AluOpType
AX = mybir.AxisListType

D = 1024
SEQ = 4096
BATCH = 4
NCORES = 8
TOK = 2048
NB = 8
G = NB * 128
NG = TOK // G
NSLOT = 6
NSCR = 6
ENG_LN_G = "dve"
ENG_LN_B = "dve"
ENG_SG = "dve"
ALPHA = 2.0 ** 0.25
EPS = 1e-5
IN_W = 5632
DFF = 4096


class Buf:
    __slots__ = ("name", "last_w", "readers")

    def __init__(self, name):
        self.name = name
        self.last_w = None
        self.readers = []


class Op:
    __slots__ = ("eng", "fn", "deps", "dma", "chan", "ndma", "signal", "ev_sem", "ev_val")

    def __init__(self, eng, fn, dma, chan, ndma):
        self.eng = eng
        self.fn = fn
        self.deps = set()
        self.dma = dma
        self.chan = chan
        self.ndma = ndma
        self.signal = False
        self.ev_sem = None
        self.ev_val = None


ENGS = ("pe", "act", "dve", "pool", "sp")


class Prog:
    def __init__(self, nc, same_engine_sync=True):
        self.nc = nc
        self.ops = {e: [] for e in ENGS}
        self.same_engine_sync = same_engine_sync

    def op(self, eng, fn, reads=(), writes=(), dma=False, chan=None, ndma=1):
        o = Op(eng, fn, dma, chan, ndma)
        for b in reads:
            if b.last_w is not None:
                o.deps.add(b.last_w)
        for b in writes:
            if b.last_w is not None:
                o.deps.add(b.last_w)
            for r in b.readers:
                o.deps.add(r)
        for b in reads:
            b.readers.append(o)
        for b in writes:
            b.last_w = o
            b.readers = []
        o.deps.discard(o)
        if dma:
            if chan is None:
                chan = (list(writes) + list(reads))[0]
            o.chan = chan
        self.ops[eng].append(o)
        return o

    @staticmethod
    def alias_after(new_bufs, old_bufs):
        acc = []
        for b in old_bufs:
            if b.last_w is not None:
                acc.append(b.last_w)
            acc += b.readers
        for nb in new_bufs:
            nb.readers = nb.readers + acc

    def build(self, final_waits=()):
        nc = self.nc
        for e in ENGS:
            for o in self.ops[e]:
                keep = set()
                for d in o.deps:
                    if d.dma:
                        keep.add(d)
                    elif d.eng == o.eng and not o.dma:
                        if o.eng == "pe":
                            continue
                        if self.same_engine_sync:
                            keep.add(d)
                    else:
                        keep.add(d)
                o.deps = keep
                for d in keep:
                    d.signal = True
        for o in final_waits:
            o.signal = True
        chans = []
        seen = set()
        for e in ENGS:
            for o in self.ops[e]:
                if o.dma and id(o.chan) not in seen:
                    seen.add(id(o.chan))
                    chans.append(o.chan)
        with contextlib.ExitStack() as st:
            esem = {e: st.enter_context(nc.semaphore("s_" + e)) for e in ENGS if e != "sp"}
            csem = {id(c): st.enter_context(nc.semaphore("c_%d" % i)) for i, c in enumerate(chans)}
            ccount = {id(c): 0 for c in chans}
            for e in ENGS:
                cnt = 0
                for o in self.ops[e]:
                    if o.dma:
                        ccount[id(o.chan)] += 16 * o.ndma
                        o.ev_sem = csem[id(o.chan)]
                        o.ev_val = ccount[id(o.chan)]
                    elif o.signal:
                        cnt += 1
                        o.ev_sem = esem[e]
                        o.ev_val = cnt
            block = st.enter_context(nc.Block())
            prog = self

            def run(eng_name, eng, tail=()):
                known = {}
                for o in prog.ops[eng_name]:
                    need = {}
                    for d in o.deps:
                        k = id(d.ev_sem)
                        if k not in need or need[k][1] < d.ev_val:
                            need[k] = (d.ev_sem, d.ev_val)
                    for k, (s, v) in need.items():
                        if known.get(k, 0) >= v:
                            continue
                        eng.wait_ge(s, v)
                        known[k] = v
                    r = o.fn(eng)
                    if o.dma:
                        rs = r if isinstance(r, (list, tuple)) else [r]
                        assert len(rs) == o.ndma
                        for ins in rs:
                            ins.then_inc(o.ev_sem, 16)
                    elif o.signal:
                        r.then_inc(o.ev_sem, 1)
                for o in tail:
                    eng.wait_ge(o.ev_sem, o.ev_val)

            @block.tensor
            def _(eng):
                run("pe", eng)

            @block.scalar
            def _(eng):
                run("act", eng)

            @block.vector
            def _(eng):
                run("dve", eng)

            @block.gpsimd
            def _(eng):
                run("pool", eng)

            @block.sync
            def _(eng):
                run("sp", eng, tail=final_waits)


class _Stop(Exception):
    pass


def build_program(stop=None):
    nc = bass.Bass("TRN2", target_bir_lowering=False)

    def din(name, shape):
        return nc.dram_tensor(name, list(shape), F32, kind="ExternalInput").ap()

    xT_d = din("xT", [D, TOK + 128])
    x_d = din("x", [TOK, D])
    w_in_d = din("w_in", [D, IN_W])
    w_bra_d = din("w_bra", [D, D])
    w_brg_d = din("w_brg", [D, D])
    w_out_d = din("w_out", [D, D])
    w_up_d = din("w_up", [D, DFF])
    w_down_d = din("w_down", [DFF, D])
    bcol_d = din("bcol", [128, 68])
    brow_d = din("brow", [128, 256])
    bzv_d = din("bzv", [1, 1024])
    lnrep_d = din("lnrep", [128, 6, 1024])
    sinks_d = din("sinks", [128, 16])
    wsT_d = din("wsT", [128, 8, 128])
    trilT_d = din("trilT", [128, 128])
    bs_d = din("bs", [1, 1024])
    amask_d = din("amask", [128, 512])
    amask0_d = din("amask0", [128, 512])
    ident_d = din("ident", [128, 128])
    y_d = nc.dram_tensor("y", [TOK, D], F32, kind="ExternalOutput").ap()

    st = contextlib.ExitStack()
    with st:
        def sb(name, shape, dt=F32):
            return st.enter_context(nc.sbuf_tensor(name, list(shape), dt))

        P = Prog(nc)

        xT = sb("xT_s", [128, 8, G + 128], BF16)
        regA = sb("regA", [128, 16, G], BF16)
        regB = sb("regB", [128, 8192], F32)
        regC = sb("regC", [128, 8, G], BF16)
        ring = [sb("ring%d" % i, [128, 4096], BF16) for i in range(NSLOT)]
        scr = [sb("scr%d" % i, [128, 1024], F32) for i in range(NSCR)]
        xres = [sb("xres%d" % i, [128, 1024], F32) for i in range(2)]
        ablk_t = [sb("ablk%d" % i, [128, 1024], BF16) for i in range(2)]
        lnbuf = sb("lnbuf", [128, 2, 1024], F32)
        bcol = sb("bcol_s", [128, 68])
        brow = sb("brow_s", [128, 256])
        sinks = sb("sinks_s", [128, 16])
        negsink = sb("negsink", [128, 16])
        negsk4 = sb("negsk4", [128, 4])
        wsTb = sb("wsTb", [128, 8, 128], BF16)
        trilT = sb("trilT_s", [128, 128])
        rows = sb("rows", [1, 2176], BF16)
        ones_row = rows[0:1, 2048:2176]
        ones_bf = sb("ones_bf", [128, 128], BF16)
        Cg = sb("Cg", [128, 8, 128], F32)
        mhalf = sb("mhalf", [128, 1])
        cm1 = sb("cm1", [128, 4])
        bes = [sb("bes%d" % i, [128, 32]) for i in range(3)]
        amask = sb("amask_s", [128, 512], BF16)
        amask0 = sb("amask0_s", [128, 512], BF16)
        identb = sb("identb", [128, 128], BF16)
        epsc = sb("epsc", [128, 1])
        NST = 10
        stat = [sb("stat%d" % i, [128, 32]) for i in range(NST)]
        ps = st.enter_context(nc.psum_tensor("ps", [128, 8, 512], F32))

        attnT = regA[:, 0:8, :]
        mixT = regA[:, 8:16, :]
        hT = regA
        regB16 = regB[:].bitcast(BF16)
        regB32 = regB[:]
        qT = regB16[:, 0:8 * G].rearrange("p (e t) -> p e t", e=8)
        kdT = regB16[:, 8 * G:8 * G + 4 * (G + 128)].rearrange("p (h t) -> p h t", h=4)
        VW = 66
        voff = 8 * G + 4 * (G + 128)
        Vaug = regB16[:, voff:voff + (NB + 1) * 4 * VW].rearrange("p (b h w) -> p b h w", b=NB + 1, h=4)
        uT = regB16[:, 0:8 * G].rearrange("p (e t) -> p e t", e=8)
        vln = regB16[:, 8 * G:16 * G].rearrange("p (b f) -> p b f", b=NB)
        x1 = regB32.rearrange("p (b f) -> p b f", b=NB)
        sgT = regC
        x1T = regC

        XS = 640
        B_xT = [Buf("xTa"), Buf("xTb")]
        B_ring = [Buf("ring%d" % i) for i in range(NSLOT)]
        B_scr = [Buf("scr%d" % i) for i in range(NSCR)]
        B_xres = [Buf("xres%d" % i) for i in range(2)]
        B_ablk = [Buf("ablk%d" % i) for i in range(2)]
        B_bes = [Buf("bes%d" % i) for i in range(3)]
        B_ln = Buf("lnbuf")
        B_stat = [Buf("stat%d" % i) for i in range(NST)]
        B_statB = [Buf("statB%d" % i) for i in range(NST)]
        B_statC = [Buf("statC%d" % i) for i in range(NST)]
        B_ps = [Buf("ps%d" % i) for i in range(8)]
        B_const = Buf("const")

        state = {"ps": 0, "scr": 0, "stat": 0, "xres": 0, "ablk": 0}

        def ps1():
            i = state["ps"]
            state["ps"] = (i + 1) % 8
            return i

        def ps2():
            i = state["ps"]
            if i % 2:
                i = (i + 1) % 8
            state["ps"] = (i + 2) % 8
            return i

        def nscr():
            i = state["scr"]
            state["scr"] = (i + 1) % NSCR
            return scr[i], B_scr[i]

        def nstat():
            i = state["stat"]
            state["stat"] = (i + 1) % NST
            return stat[i], B_stat[i]

        def nstat3():
            i = state["stat"]
            state["stat"] = (i + 1) % NST
            return stat[i], B_stat[i], B_statB[i], B_statC[i]

        def ld(eng, dst, src, bufs):
            return P.op(eng, lambda e: e.dma_start(out=dst, in_=src), writes=bufs, dma=True)

        B_c = {n: Buf(n) for n in ["bcol", "brow", "sinks", "negsink", "wsTb", "trilT", "bs_hi", "bs_lo",
                                   "ones", "amask", "amask0", "identb", "eps"]}
        ld("sp", bcol[:], bcol_d[:, :], [B_c["bcol"]])
        ld("sp", brow[:], brow_d[:, :], [B_c["brow"]])
        ld("sp", sinks[:], sinks_d[:, :], [B_c["sinks"]])
        ld("sp", trilT[:], trilT_d[:, :], [B_c["trilT"]])
        sm0, bsm0 = nscr()
        ld("sp", sm0[:, 0:512], amask_d[:, :], [bsm0])
        P.op("dve", lambda e: e.tensor_copy(out=amask[:], in_=sm0[:, 0:512]), reads=[bsm0], writes=[B_c["amask"]])
        sm1, bsm1 = nscr()
        ld("sp", sm1[:, 0:512], amask0_d[:, :], [bsm1])
        P.op("dve", lambda e: e.tensor_copy(out=amask0[:], in_=sm1[:, 0:512]), reads=[bsm1], writes=[B_c["amask0"]])
        s0, bs0 = nscr()
        ld("sp", s0[:, 0:128], ident_d[:, :], [bs0])
        P.op("dve", lambda e: e.tensor_copy(out=identb[:], in_=s0[:, 0:128]), reads=[bs0], writes=[B_c["identb"]])
        s1, bs1 = nscr()
        ld("sp", s1[:, :], wsT_d.rearrange("p g t -> p (g t)"), [bs1])
        P.op("dve", lambda e: e.tensor_tensor(out=wsTb[:], in0=s1[:].rearrange("p (g t) -> p g t", g=8),
                                              in1=trilT[:].unsqueeze(1).broadcast_to([128, 8, 128]), op=ALU.mult),
             reads=[bs1, B_c["trilT"]], writes=[B_c["wsTb"]])
        for n_ in ["rows", "ones_bf", "Cg", "mhalf"]:
            B_c[n_] = Buf(n_)
        P.op("dve", lambda e: e.memset(rows[0:1, 2048:2176], 1.0), writes=[B_c["ones"]])
        P.op("dve", lambda e: e.memset(ones_bf[:], 1.0), writes=[B_c["ones_bf"]])
        P.op("dve", lambda e: e.memset(epsc[:], EPS), writes=[B_c["eps"]])
        P.op("dve", lambda e: e.memset(mhalf[:], -0.5), writes=[B_c["mhalf"]])
        P.op("dve", lambda e: e.memset(cm1[:], -1.0), writes=[B_c["mhalf"]])
        s2, bs2 = nscr()
        ld("sp", s2[0:1, :], bzv_d[:, :], [bs2])
        s3, bs3 = nscr()
        P.op("dve", lambda e: e.tensor_copy(out=rows[0:1, 0:1024], in_=s2[0:1, :]), reads=[bs2], writes=[B_c["rows"]])
        P.op("dve", lambda e: e.tensor_tensor(out=s3[0:1, :], in0=s2[0:1, :], in1=rows[0:1, 0:1024], op=ALU.subtract),
             reads=[bs2, B_c["rows"]], writes=[bs3])
        P.op("dve", lambda e: e.tensor_copy(out=rows[0:1, 1024:2048], in_=s3[0:1, :]), reads=[bs3], writes=[B_c["rows"]])
        s4, bs4 = nscr()
        ld("sp", s4[0:1, :], bs_d[:, :], [bs4])
        s5, bs5 = nscr()
        s5b = s5[:].bitcast(BF16)
        s6, bs6 = nscr()
        P.op("dve", lambda e: e.tensor_copy(out=s5b[0:1, 0:1024], in_=s4[0:1, :]), reads=[bs4], writes=[bs5])
        P.op("dve", lambda e: e.tensor_tensor(out=s6[0:1, :], in0=s4[0:1, :], in1=s5b[0:1, 0:1024], op=ALU.subtract),
             reads=[bs4, bs5], writes=[bs6])
        P.op("dve", lambda e: e.tensor_copy(out=s5b[0:1, 1024:2048], in_=s6[0:1, :]), reads=[bs6, bs5], writes=[bs5])
        for gi in range(8):
            b = ps1()

            def fC(e, gi=gi, b=b):
                e.matmul(ps[:, b, 0:128], lhsT=ones_bf[:, :], rhs=wsTb[:, gi, :], start=True, stop=True)
                e.matmul(ps[:, b, 128:256], lhsT=ones_row, rhs=s5b[0:1, gi * 128:(gi + 1) * 128], start=True, stop=False)
                return e.matmul(ps[:, b, 128:256], lhsT=ones_row, rhs=s5b[0:1, 1024 + gi * 128:1024 + (gi + 1) * 128], start=False, stop=True)
            P.op("pe", fC, reads=[B_c["ones_bf"], B_c["wsTb"], B_c["ones"], bs5], writes=[B_ps[b]])
            P.op("dve", (lambda gi, b: lambda e: e.tensor_copy(out=Cg[:, gi, :], in_=ps[:, b, 128:256]))(gi, b), reads=[B_ps[b]], writes=[B_c["Cg"]])
            P.op("dve", (lambda gi, b: lambda e: e.scalar_tensor_tensor(out=Cg[:, gi, :], in0=ps[:, b, 0:128], scalar=bcol[:, 44 + gi:45 + gi],
                                                                       in1=Cg[:, gi, :], op0=ALU.mult, op1=ALU.add))(gi, b),
                 reads=[B_ps[b], B_c["bcol"], B_c["Cg"]], writes=[B_c["Cg"]])
        P.op("dve", lambda e: e.tensor_scalar(out=negsink[:], in0=sinks[:], scalar1=-1.0, scalar2=None, op0=ALU.mult),
             reads=[B_c["sinks"]], writes=[B_c["negsink"]])
        P.op("dve", lambda e: e.tensor_reduce(out=negsk4[:], in_=negsink[:].rearrange("p (h j) -> p h j", h=4), axis=AX.X, op=ALU.min),
             reads=[B_c["negsink"]], writes=[B_c["negsink"]])

        w_in_v = w_in_d.rearrange("(k p) n -> p k n", p=128)
        w_bra_v = w_bra_d.rearrange("(k p) n -> p k n", p=128)
        w_brg_v = w_brg_d.rearrange("(k p) n -> p k n", p=128)
        w_out_v = w_out_d.rearrange("(k p) n -> p k n", p=128)
        w_up_v = w_up_d.rearrange("(k p) n -> p k n", p=128)
        w_down_v = w_down_d.rearrange("(f p) n -> p f n", p=128)

        def col_chunk(view, c0, ncols=512):
            def f(e, slot):
                dst = slot[:, 0:8 * ncols].rearrange("p (k n) -> p k n", k=8)
                return [e.dma_start(out=dst, in_=view[:, :, c0:c0 + ncols])]
            return f, 1

        def pair_chunk(viewA, cA, viewB, cB):
            def f(e, slot):
                dA = slot[:, 0:2048].rearrange("p (k n) -> p k n", k=8)
                dB = slot[:, 2048:4096].rearrange("p (k n) -> p k n", k=8)
                return [e.dma_start(out=dA, in_=viewA[:, :, cA:cA + 256]), e.dma_start(out=dB, in_=viewB[:, :, cB:cB + 256])]
            return f, 2

        def kdup_chunk():
            def f(e, slot):
                dst = slot[:, :].rearrange("p (k h r d) -> p k h r d", k=8, h=4, r=2)
                src = w_in_v[:, :, 1024:1280].rearrange("p k (h d) -> p k h d", h=4)
                return [e.dma_start(out=dst[:, :, h, r, :], in_=src[:, :, h, :]) for h in range(4) for r in range(2)]
            return f, 8

        def down_chunk(f0):
            def f(e, slot):
                dst = slot[:, :].rearrange("p (f n) -> p f n", f=4)
                return [e.dma_start(out=dst, in_=w_down_v[:, f0:f0 + 4, :])]
            return f, 1

        stream = []
        for g in range(NG):
            stream += [((g, "q", i), col_chunk(w_in_v, i * 512)) for i in range(2)]
            stream += [((g, "k", 0), col_chunk(w_in_v, 1024, 256))]
            stream += [((g, "v", 0), col_chunk(w_in_v, 1280, 256))]
            stream += [((g, "zv", i), col_chunk(w_in_v, 2560 + i * 512)) for i in range(2)]
            stream += [((g, "zu", i), col_chunk(w_in_v, 1536 + i * 512)) for i in range(2)]
            for i in range(4):
                stream += [((g, "pab", i), pair_chunk(w_bra_v, i * 256, w_brg_v, i * 256)),
                           ((g, "pgg", i), pair_chunk(w_in_v, 3584 + i * 256, w_in_v, 4608 + i * 256))]
            stream += [((g, "o", i), col_chunk(w_out_v, i * 512)) for i in range(2)]
            for fh in range(2):
                stream += [((g, "up", fh * 4 + i), col_chunk(w_up_v, (fh * 4 + i) * 512)) for i in range(4)]
                stream += [((g, "dn", fh * 4 + i), down_chunk((fh * 4 + i) * 4)) for i in range(4)]
        rs = {"next": 0, "free": list(range(NSLOT)), "where": {}}

        def ring_pump(limit=None):
            n_ = 0
            while rs["free"] and rs["next"] < len(stream) and (limit is None or n_ < limit):
                n_ += 1
                cid, (loader, ndma) = stream[rs["next"]]
                rs["next"] += 1
                s = rs["free"].pop(0)
                rs["where"][cid] = s
                P.op("pool", (lambda loader, s: lambda e: loader(e, ring[s]))(loader, s),
                     writes=[B_ring[s]], dma=True, ndma=ndma)

        def acquire(cid):
            ring_pump()
            assert cid in rs["where"], cid
            s = rs["where"][cid]
            return ring[s], B_ring[s]

        def release(cid):
            s = rs["where"].pop(cid)
            rs["free"].append(s)
            ring_pump()

        xT_v = xT_d.rearrange("(k p) t -> p k t", p=128)

        def load_xT(g, part):
            lo, hi = (0, XS) if part == 0 else (XS, G + 128)
            P.op("pool", lambda e: [e.dma_start(out=xT[:, :, lo:hi], in_=xT_v[:, :, g * G + lo:g * G + hi])],
                 writes=[B_xT[part]], dma=True, ndma=1)

        def mm_group(insts_fn, reads, pbufs):
            return P.op("pe", insts_fn, reads=reads, writes=pbufs)

        def ln_stats(src_ap, src_buf):
            stt, bst = nstat()
            P.op("dve", lambda e: e.bn_stats(out=stt[:, 0:6], in_=src_ap[:, 0:512]), reads=[src_buf], writes=[bst])
            P.op("dve", lambda e: e.bn_stats(out=stt[:, 6:12], in_=src_ap[:, 512:1024]), reads=[src_buf], writes=[bst])
            P.op("dve", lambda e: e.bn_aggr(out=stt[:, 12:14], in_=stt[:, 0:12]), reads=[bst], writes=[bst])
            P.op("pool", lambda e: e.tensor_tensor(out=stt[:, 14:15], in0=stt[:, 13:14], in1=epsc[:, 0:1], op=ALU.add),
                 reads=[bst, B_c["eps"]], writes=[bst])
            P.op("pool", lambda e: e.tensor_tensor(out=stt[:, 15:16], in0=stt[:, 14:15], in1=mhalf[:, 0:1], op=ALU.pow),
                 reads=[bst, B_c["mhalf"]], writes=[bst])
            P.op("pool", lambda e: e.tensor_tensor(out=stt[:, 17:18], in0=stt[:, 12:13], in1=cm1[:, 0:1], op=ALU.mult),
                 reads=[bst, B_c["mhalf"]], writes=[bst])
            P.op("pool", lambda e: e.tensor_tensor(out=stt[:, 16:17], in0=stt[:, 17:18], in1=stt[:, 15:16], op=ALU.mult),
                 reads=[bst], writes=[bst])
            return stt, bst

        def ln_normalize(src_ap, src_buf, stt, bst):
            P.op("act", lambda e: e.activation(out=src_ap, in_=src_ap, func=AF.Identity, bias=stt[:, 16:17], scale=stt[:, 15:16]),
                 reads=[src_buf, bst], writes=[src_buf])

        def ln_affine(src_ap, src_buf, gamma_ap, beta_ap, dst_ap, dst_bufs):
            P.op(ENG_LN_G, lambda e: e.tensor_tensor(out=src_ap, in0=src_ap, in1=gamma_ap, op=ALU.mult),
                 reads=[src_buf, B_ln], writes=[src_buf])
            P.op(ENG_LN_B, lambda e: e.tensor_tensor(out=dst_ap, in0=src_ap, in1=beta_ap, op=ALU.add),
                 reads=[src_buf, B_ln], writes=dst_bufs)

        def ln_apply(src_ap, src_buf, stt, bst, gamma_ap, beta_ap, dst_ap, dst_bufs):
            P.op("act", lambda e: e.activation(out=src_ap, in_=src_ap, func=AF.Identity, bias=stt[:, 16:17], scale=stt[:, 15:16]),
                 reads=[src_buf, bst], writes=[src_buf])
            P.op(ENG_LN_G, lambda e: e.tensor_tensor(out=src_ap, in0=src_ap, in1=gamma_ap, op=ALU.mult),
                 reads=[src_buf, B_ln], writes=[src_buf])
            P.op(ENG_LN_B, lambda e: e.tensor_tensor(out=dst_ap, in0=src_ap, in1=beta_ap, op=ALU.add),
                 reads=[src_buf, B_ln], writes=dst_bufs)

        def layer_norm_rows(src_ap, src_buf, gamma_ap, beta_ap, dst_ap, dst_bufs):
            stt, bst = ln_stats(src_ap, src_buf)
            ln_apply(src_ap, src_buf, stt, bst, gamma_ap, beta_ap, dst_ap, dst_bufs)

        def run_pipeline(n, stages, skew, hook=None):
            for t in range(n + max(skew)):
                for fn, sk in zip(stages, skew):
                    u = t - sk
                    if 0 <= u < n:
                        fn(u)
                if hook is not None:
                    hook(t)

        def load_ln(idx):
            P.op("sp", lambda e: e.dma_start(out=lnbuf[:], in_=lnrep_d[:, 2 * idx:2 * idx + 2, :]), writes=[B_ln], dma=True)

        def proj_fm(cids, rhs_ap_fn, rhs_bufs, evac_fn, ncols_chunks=8, post=None):
            for e_ in range(ncols_chunks):
                slot, bslot = acquire(cids[e_ // 4])
                wv = slot[:, :].rearrange("p (k n) -> p k n", k=8)
                for th in range(2):
                    b = ps1()

                    def f(e, wv=wv, e_=e_, th=th, b=b):
                        r = None
                        for k in range(8):
                            r = e.matmul(ps[:, b, :], lhsT=wv[:, k, (e_ % 4) * 128:(e_ % 4 + 1) * 128],
                                         rhs=rhs_ap_fn(k, th), start=(k == 0), stop=(k == 7))
                        return r
                    mm_group(f, [bslot] + (rhs_bufs(th) if callable(rhs_bufs) else rhs_bufs), [B_ps[b]])
                    evac_fn(e_, th, b)
                if post is not None:
                    post(e_)
                if e_ % 4 == 3:
                    release(cids[e_ // 4])

        out_ops = []

        def chk(level):
            if stop is not None and stop == level:
                raise _Stop()

        load_xT(0, 0)
        ring_pump(1)
        load_xT(0, 1)
        ring_pump()
        try:
          for g in range(NG):
            chk(0)
            B_qT = [Buf("qT%d" % e) for e in range(8)]
            B_kdT = Buf("kdT")
            B_V = Buf("V")
            B_attnT = Buf("attnT")
            if g > 0:
                for e_ in range(8):
                    Prog.alias_after([B_qT[e_]], [prev_B_x1[e_ // 2]])
                Prog.alias_after([B_kdT], prev_B_x1[4:7])
                Prog.alias_after([B_V], prev_B_x1[6:8])
                Prog.alias_after([B_attnT], prev_B_hT)

            def evac_q(e_, th, b):
                P.op("act", lambda e: e.activation(out=qT[:, e_, th * 512:(th + 1) * 512], in_=ps[:, b, :], func=AF.Identity,
                                                   bias=bcol[:, e_:e_ + 1], scale=1.0),
                     reads=[B_ps[b], B_c["bcol"]], writes=[B_qT[e_]])
            proj_fm([(g, "q", 0), (g, "q", 1)], lambda k, th: xT[:, k, 128 + th * 512:128 + (th + 1) * 512], lambda th: [B_xT[th]], evac_q)

            slot, bslot = acquire((g, "k", 0))
            wk = slot[:, 0:2048].rearrange("p (k n) -> p k n", k=8)
            for c_ in range(2):
                for (t0, n) in ((0, 512), (512, 512), (1024, 128)):
                    b = ps1()

                    def f(e, c_=c_, t0=t0, n=n, b=b, wk=wk):
                        r = None
                        for k in range(8):
                            r = e.matmul(ps[:, b, 0:n], lhsT=wk[:, k, c_ * 128:(c_ + 1) * 128], rhs=xT[:, k, t0:t0 + n], start=(k == 0), stop=(k == 7))
                        return r
                    mm_group(f, [bslot] + ([B_xT[0]] if t0 + n <= XS else (B_xT if t0 < XS else [B_xT[1]])), [B_ps[b]])
                    P.op("act", (lambda c_, t0, n, b: lambda e: e.activation(out=kdT[0:64, 2 * c_, t0:t0 + n], in_=ps[0:64, b, 0:n], func=AF.Identity,
                                                                            bias=bcol[0:64, 8 + c_:9 + c_], scale=1.0))(c_, t0, n, b),
                         reads=[B_ps[b], B_c["bcol"]], writes=[B_kdT])
                    P.op("act", (lambda c_, t0, n, b: lambda e: e.activation(out=kdT[64:128, 2 * c_ + 1, t0:t0 + n], in_=ps[64:128, b, 0:n], func=AF.Identity,
                                                                            bias=bcol[64:128, 8 + c_:9 + c_], scale=1.0))(c_, t0, n, b),
                         reads=[B_ps[b], B_c["bcol"]], writes=[B_kdT])
            P.op("sp", lambda e: [e.dma_start(out=kdT[64:128, 0, :], in_=kdT[0:64, 0, :]),
                                  e.dma_start(out=kdT[0:64, 1, :], in_=kdT[64:128, 1, :]),
                                  e.dma_start(out=kdT[64:128, 2, :], in_=kdT[0:64, 2, :]),
                                  e.dma_start(out=kdT[0:64, 3, :], in_=kdT[64:128, 3, :])],
                 reads=[B_kdT], writes=[B_kdT], dma=True, ndma=4)
            release((g, "k", 0))

            slot, bslot = acquire((g, "v", 0))
            wvv = slot[:, 0:2048].rearrange("p (k n) -> p k n", k=8)
            P.op("dve", lambda e: e.memset(Vaug[:, :, :, 64:65], 1.0), writes=[B_V])
            for blk in range(NB + 1):
                b = ps1()

                def f(e, blk=blk, b=b, wvv=wvv):
                    r = None
                    for k in range(8):
                        r = e.matmul(ps[:, b, 0:256], lhsT=xT[:, k, blk * 128:(blk + 1) * 128], rhs=wvv[:, k, :], start=(k == 0), stop=(k == 7))
                    return r
                mm_group(f, [bslot, B_xT[0] if (blk + 1) * 128 <= XS else B_xT[1]], [B_ps[b]])
                P.op("dve", (lambda blk, b: lambda e: e.tensor_tensor(out=Vaug[:, blk, :, 0:64],
                                                                      in0=ps[:, b, 0:256].rearrange("p (h d) -> p h d", h=4),
                                                                      in1=brow[:, 0:256].rearrange("p (h d) -> p h d", h=4), op=ALU.add))(blk, b),
                     reads=[B_ps[b], B_c["brow"]], writes=[B_V])
            release((g, "v", 0))

            chk(1)
            units = [(blk, hk) for blk in range(NB) for hk in range(4)]
            ctx = [dict() for _ in units]
            ablk_of = {}

            def att_A(u):
                blk, hk = units[u]
                c = ctx[u]
                if hk == 0:
                    i = state["ablk"]
                    state["ablk"] = (i + 1) % 2
                    ablk_of[blk] = (ablk_t[i], B_ablk[i])
                b2 = 2 * (u % 2)

                mask_t = amask0 if (g == 0 and blk == 0) else amask
                mask_b = B_c["amask0"] if (g == 0 and blk == 0) else B_c["amask"]

                def f(e, blk=blk, hk=hk, b2=b2):
                    r = None
                    for cb in range(2):
                        for b_ in range(2):
                            e.matmul(ps[:, b2 + b_, cb * 256:(cb + 1) * 256], lhsT=identb[:], rhs=mask_t[:, 0:256], start=True, stop=False)
                        for b_ in range(2):
                            j = cb * 2 + b_
                            hq = hk * 4 + j
                            base = (hq % 2) * 64
                            r = e.matmul(ps[:, b2 + b_, cb * 256:(cb + 1) * 256],
                                         lhsT=qT[base:base + 64, hq // 2, blk * 128:(blk + 1) * 128],
                                         rhs=kdT[base:base + 64, hk, blk * 128:blk * 128 + 256], start=False, stop=True)
                    return r
                mm_group(f, [B_qT[2 * hk], B_qT[2 * hk + 1], B_kdT, mask_b, B_c["identb"]], [B_ps[b2], B_ps[b2 + 1]])
                stt, bst, bstB, bstC = nstat3()
                Sall = ps[:, b2:b2 + 2, :].rearrange("p b n -> p (b n)")
                P.op("dve", lambda e: e.tensor_reduce(out=stt[:, 0:1], in_=Sall, axis=AX.X, op=ALU.max),
                     reads=[B_ps[b2], B_ps[b2 + 1]], writes=[bst])
                P.op("dve", lambda e: e.scalar_tensor_tensor(out=stt[:, 1:2], in0=stt[:, 0:1], scalar=-0.125,
                                                             in1=negsk4[:, hk:hk + 1], op0=ALU.mult, op1=ALU.min),
                     reads=[bst, B_c["negsink"]], writes=[bst])
                praw, bpraw = nscr()
                praw16 = praw[:].bitcast(BF16)
                bpts = Buf("pts")
                P.op("act", lambda e: e.activation(out=praw16[:, 0:1024], in_=Sall, func=AF.Exp, bias=stt[:, 1:2], scale=0.125),
                     reads=[B_ps[b2], B_ps[b2 + 1], bst], writes=[bpraw])
                bes_t, bbes = bes[blk % 3], B_bes[blk % 3]
                P.op("pool", lambda e: e.tensor_tensor(out=bes_t[:, hk * 4:hk * 4 + 4], in0=negsink[:, hk * 4:hk * 4 + 4],
                                                       in1=stt[:, 1:2].broadcast_to([128, 4]), op=ALU.subtract),
                     reads=[bst, B_c["negsink"]], writes=[bbes])
                if hk == 3:
                    P.op("act", lambda e: e.activation(out=bes_t[:, 16:32], in_=bes_t[:, 0:16], func=AF.Exp, scale=-1.0), reads=[bbes], writes=[bbes])
                c.update(bes_t=bes_t, bbes=bbes)
                c.update(bstB=bstB, bstC=bstC, bpts=bpts)
                c.update(stt=stt, bst=bst, praw16=praw16, bpraw=bpraw)

            def att_B1(u):
                blk, hk = units[u]
                c = ctx[u]
                praw16, bpraw = c["praw16"], c["bpraw"]
                bt = 4 + (u % 2)
                PTp = ps[:, bt, :].bitcast(BF16).rearrange("p (j c q) -> p j c q", j=4, c=2)

                def ft(e):
                    r = None
                    for j in range(4):
                        for cc in range(2):
                            pj = (j % 2) * 2 + j // 2
                            r = e.transpose(out=PTp[:, j, cc, :], in_=praw16[:, pj * 256 + cc * 128:pj * 256 + (cc + 1) * 128], identity=identb[:])
                    return r
                mm_group(ft, [bpraw, B_c["identb"]], [B_ps[bt]])
                c.update(bt=bt, PTp=PTp)

            def att_B2(u):
                blk, hk = units[u]
                c = ctx[u]
                praw16, bpraw, bt, PTp = c["praw16"], c["bpraw"], c["bt"], c["PTp"]
                PTs = praw16[:, 1024:2048].rearrange("p (j c q) -> p j c q", j=4, c=2)
                c.update(PTs=PTs)
                P.op("act", lambda e: e.activation(out=praw16[:, 1024:2048], in_=ps[:, bt, :].bitcast(BF16), func=AF.Copy),
                     reads=[B_ps[bt], bpraw], writes=[c["bpts"]])

            def att_B3(u):
                blk, hk = units[u]
                c = ctx[u]
                bpraw, PTs = c["bpraw"], c["PTs"]
                bo = 6
                Ov = ps[:, bo, :].rearrange("p (j w) -> p j w", w=128)

                def fo(e):
                    r = None
                    for j in range(4):
                        for cc in range(2):
                            r = e.matmul(Ov[:, j, 0:65], lhsT=PTs[:, j, cc, :], rhs=Vaug[:, blk + cc, hk, 0:65], start=(cc == 0), stop=(cc == 1))
                    return r
                mm_group(fo, [bpraw, c["bpts"], B_V], [B_ps[bo]])
                c.update(bo=bo, Ov=Ov)

            def att_C1a(u):
                c = ctx[u]
                stt, bo, Ov = c["stt"], c["bo"], c["Ov"]
                bstB, bstC = c["bstB"], c["bstC"]
                hk_ = units[u][1]
                bes_t, bbes = c["bes_t"], c["bbes"]
                P.op("dve", lambda e: e.tensor_tensor(out=stt[:, 16:20], in0=Ov[:, :, 64:65].rearrange("p j o -> p (j o)"),
                                                      in1=bes_t[:, 16 + hk_ * 4:20 + hk_ * 4], op=ALU.add),
                     reads=[B_ps[bo], bbes], writes=[bstC])
                P.op("dve", lambda e: e.reciprocal(out=stt[:, 20:24], in_=stt[:, 16:20]), reads=[bstC], writes=[bstC])

            def att_C(u):
                blk, hk = units[u]
                c = ctx[u]
                stt, bst, bo, Ov = c["stt"], c["bst"], c["bo"], c["Ov"]
                ablk16, bablk = ablk_of[blk]
                bstB, bstC = c["bstB"], c["bstC"]
                P.op("dve", lambda e: e.tensor_tensor(
                    out=ablk16[:, hk * 256:(hk + 1) * 256].rearrange("p (j d) -> p j d", j=4), in0=Ov[:, :, 0:64],
                    in1=stt[:, 20:24].unsqueeze(2).broadcast_to([128, 4, 64]), op=ALU.mult),
                    reads=[B_ps[bo], bstC], writes=[bablk])

            def att_C2(u):
                blk, hk = units[u]
                ablk16, bablk = ablk_of[blk]
                if hk == 3:
                    bt = 7
                    ATp = ps[:, bt, :].bitcast(BF16).rearrange("p (e q) -> p e q", e=8)

                    def fa(e):
                        r = None
                        for e_ in range(8):
                            r = e.transpose(out=ATp[:, e_, :], in_=ablk16[:, e_ * 128:(e_ + 1) * 128], identity=identb[:])
                        return r
                    mm_group(fa, [bablk, B_c["identb"]], [B_ps[bt]])
                    P.op("dve", lambda e: e.tensor_copy(out=attnT[:, :, blk * 128:(blk + 1) * 128], in_=ATp),
                         reads=[B_ps[bt]], writes=[B_attnT])

            run_pipeline(len(units), [att_B2, att_C1a, att_C, att_A, att_B1, att_B3, att_C2], [3, 5, 5, 0, 2, 4, 6])

            chk(2)
            B_uT = [Buf("uT%d" % e) for e in range(8)]
            B_vln = [Buf("vln%d" % b) for b in range(NB)]
            Prog.alias_after(B_uT + B_vln, B_qT + [B_kdT, B_V])
            B_sgT = [Buf("sgT%d" % q_) for q_ in range(NB // 4)]
            if g > 0:
                Prog.alias_after(B_sgT, prev_B_x1T)

            sl0, bsl0 = acquire((g, "zv", 0))
            sl1, bsl1 = acquire((g, "zv", 1))
            wz = [sl0[:, :].rearrange("p (k n) -> p k n", k=8), sl1[:, :].rearrange("p (k n) -> p k n", k=8)]
            zctx = [dict() for _ in range(NB)]

            def zv_A(blk):
                b2 = ps2()

                def f(e, wz=wz):
                    r = None
                    for half in range(2):
                        o = ps[:, b2 + half, :]
                        e.matmul(o, lhsT=ones_row, rhs=rows[0:1, half * 512:(half + 1) * 512], start=True, stop=False)
                        e.matmul(o, lhsT=ones_row, rhs=rows[0:1, 1024 + half * 512:1024 + (half + 1) * 512], start=False, stop=False)
                        for k in range(8):
                            r = e.matmul(o, lhsT=xT[:, k, 128 + blk * 128:128 + (blk + 1) * 128], rhs=wz[half][:, k, :],
                                         start=False, stop=(k == 7))
                    return r
                mm_group(f, [bsl0, bsl1, B_xT[0] if 128 + (blk + 1) * 128 <= XS else B_xT[1], B_c["rows"], B_c["ones"]], [B_ps[b2], B_ps[b2 + 1]])
                zt, bzt = nscr()
                P.op("act", lambda e: e.activation(out=zt[:, :], in_=ps[:, b2:b2 + 2, :].rearrange("p b n -> p (b n)"), func=AF.Gelu_apprx_tanh),
                     reads=[B_ps[b2], B_ps[b2 + 1]], writes=[bzt])
                stt, bst = ln_stats(zt[:, :], bzt)
                zctx[blk].update(zt=zt, bzt=bzt, stt=stt, bst=bst)

            def zv_B(blk):
                c = zctx[blk]
                zt, bzt, stt, bst = c["zt"], c["bzt"], c["stt"], c["bst"]
                P.op("act", lambda e: e.activation(out=vln[:, blk, :], in_=zt[:, :], func=AF.Identity, bias=stt[:, 16:17], scale=stt[:, 15:16]),
                     reads=[bzt, bst], writes=[B_vln[blk]])

            def spatial(gi, quad):
                b = ps1()

                def f(e):
                    r = None
                    for b4 in range(4):
                        blk = quad * 4 + b4
                        r = e.matmul(ps[:, b, b4 * 128:(b4 + 1) * 128], lhsT=vln[:, blk, gi * 128:(gi + 1) * 128], rhs=wsTb[:, gi, :],
                                     start=True, stop=True)
                    return r
                mm_group(f, [B_vln[quad * 4 + i] for i in range(4)] + [B_c["wsTb"]], [B_ps[b]])
                tq, btq = nscr()
                P.op("dve", lambda e: e.scalar_tensor_tensor(
                    out=tq[:, 0:512].rearrange("p (a t) -> p a t", a=4), in0=ps[:, b, :].rearrange("p (a t) -> p a t", a=4),
                    scalar=bcol[:, 36 + gi:37 + gi], in1=Cg[:, gi, :].unsqueeze(1).broadcast_to([128, 4, 128]),
                    op0=ALU.mult, op1=ALU.add),
                    reads=[B_ps[b], B_c["bcol"], B_c["Cg"]], writes=[btq])
                P.op(ENG_SG, lambda e: e.tensor_tensor(out=sgT[:, gi, quad * 512:(quad + 1) * 512], in0=tq[:, 0:512],
                                                       in1=uT[:, gi, quad * 512:(quad + 1) * 512], op=ALU.mult),
                     reads=[btq, B_uT[gi]], writes=[B_sgT[quad]])

            def evac_u(e_, th, b):
                P.op("act", lambda e: e.activation(out=uT[:, e_, th * 512:(th + 1) * 512], in_=ps[:, b, :], func=AF.Gelu_apprx_tanh,
                                                   bias=bcol[:, 12 + e_:13 + e_], scale=1.0),
                     reads=[B_ps[b], B_c["bcol"]], writes=[B_uT[e_]])
            run_pipeline(NB, [zv_A, zv_B], [0, 1])
            release((g, "zv", 0))
            release((g, "zv", 1))

            def post_u(e_):
                for quad in range(NB // 4):
                    spatial(e_, quad)
            proj_fm([(g, "zu", 0), (g, "zu", 1)], lambda k, th: xT[:, k, 128 + th * 512:128 + (th + 1) * 512], lambda th: [B_xT[th]], evac_u, post=post_u)

            chk(3)
            B_mixT = Buf("mixT")
            if g > 0:
                Prog.alias_after([B_mixT], prev_B_hT)
            for e_ in range(8):
                i = e_ // 2
                slAB = acquire((g, "pab", i))
                slGG = acquire((g, "pgg", i))
                sls = [slAB, slAB, slGG, slGG]
                wvs = [sl[0][:, hf * 2048:(hf + 1) * 2048].rearrange("p (k n) -> p k n", k=8) for sl, hf in zip(sls, (0, 1, 0, 1))]
                for th in range(2):
                    rhs_list = [(attnT, 0, [B_attnT]), (sgT, 0, [B_sgT[th]]), (xT, 128, [B_xT[th]]), (xT, 128, [B_xT[th]])]
                    bb = [ps1() for _ in range(4)]
                    for m in range(4):
                        src, off, rb = rhs_list[m]

                        def f(e, m=m, src=src, off=off, b=bb[m], wv=wvs[m], e_=e_, th=th):
                            r = None
                            for k in range(8):
                                r = e.matmul(ps[:, b, :], lhsT=wv[:, k, (e_ % 2) * 128:(e_ % 2 + 1) * 128],
                                             rhs=src[:, k, off + th * 512:off + (th + 1) * 512], start=(k == 0), stop=(k == 7))
                            return r
                        mm_group(f, [sls[m][1]] + rb, [B_ps[bb[m]]])
                    sg_, bsg_ = nscr()
                    P.op("act", (lambda sg_, b, e_: lambda e: e.activation(out=sg_[:, 0:512], in_=ps[:, b, :], func=AF.Sigmoid,
                                                                          bias=bcol[:, 20 + e_:21 + e_], scale=1.0))(sg_, bb[2], e_),
                         reads=[B_ps[bb[2]], B_c["bcol"]], writes=[bsg_])
                    P.op("act", (lambda sg_, b, e_: lambda e: e.activation(out=sg_[:, 512:1024], in_=ps[:, b, :], func=AF.Sigmoid,
                                                                          bias=bcol[:, 28 + e_:29 + e_], scale=1.0))(sg_, bb[3], e_),
                         reads=[B_ps[bb[3]], B_c["bcol"]], writes=[bsg_])
                    P.op("dve", (lambda sg_, b: lambda e: e.tensor_tensor(out=sg_[:, 0:512], in0=ps[:, b, :], in1=sg_[:, 0:512], op=ALU.mult))(sg_, bb[0]),
                         reads=[B_ps[bb[0]], bsg_], writes=[bsg_])
                    P.op("dve", (lambda sg_, b: lambda e: e.tensor_tensor(out=sg_[:, 512:1024], in0=ps[:, b, :], in1=sg_[:, 512:1024], op=ALU.mult))(sg_, bb[1]),
                         reads=[B_ps[bb[1]], bsg_], writes=[bsg_])
                    P.op("dve", (lambda sg_, e_, th: lambda e: e.tensor_tensor(out=mixT[:, e_, th * 512:(th + 1) * 512], in0=sg_[:, 0:512],
                                                                              in1=sg_[:, 512:1024], op=ALU.add))(sg_, e_, th),
                         reads=[bsg_], writes=[B_mixT])
                if e_ % 2 == 1:
                    release((g, "pab", i))
                    release((g, "pgg", i))
            if g + 1 < NG:
                load_xT(g + 1, 0)
                load_xT(g + 1, 1)

            chk(4)
            B_x1 = [Buf("x1_%d" % b) for b in range(NB)]
            Prog.alias_after(B_x1, B_uT + B_vln)
            B_x1T = [Buf("x1T%d" % e_) for e_ in range(8)]
            Prog.alias_after(B_x1T, B_sgT)
            load_ln(1)
            so0, bso0 = acquire((g, "o", 0))
            so1, bso1 = acquire((g, "o", 1))
            wo = [so0[:, :].rearrange("p (k n) -> p k n", k=8), so1[:, :].rearrange("p (k n) -> p k n", k=8)]
            octx = [dict() for _ in range(NB)]

            def o_A(blk):
                b2 = ps2()

                def f(e, wo=wo):
                    r = None
                    for half in range(2):
                        for k in range(8):
                            r = e.matmul(ps[:, b2 + half, :], lhsT=mixT[:, k, blk * 128:(blk + 1) * 128], rhs=wo[half][:, k, :],
                                         start=(k == 0), stop=(k == 7))
                    return r
                mm_group(f, [bso0, bso1, B_mixT], [B_ps[b2], B_ps[b2 + 1]])
                xi = state["xres"]
                state["xres"] = 1 - xi
                row0 = g * G + blk * 128
                P.op("sp", lambda e: e.dma_start(out=xres[xi][:], in_=x_d[row0:row0 + 128, :]), writes=[B_xres[xi]], dma=True)
                r1, br1 = scr[blk % 3], B_scr[blk % 3]
                P.op("dve", lambda e: e.scalar_tensor_tensor(out=r1[:, :], in0=xres[xi][:], scalar=ALPHA,
                                                             in1=ps[:, b2:b2 + 2, :].rearrange("p b n -> p (b n)"),
                                                             op0=ALU.mult, op1=ALU.add),
                     reads=[B_xres[xi], B_ps[b2], B_ps[b2 + 1]], writes=[br1])
                stt, bst = ln_stats(r1[:, :], br1)
                octx[blk].update(r1=r1, br1=br1, stt=stt, bst=bst)

            def o_B(blk):
                c = octx[blk]
                r1, br1, stt, bst = c["r1"], c["br1"], c["stt"], c["bst"]
                P.op("act", lambda e: e.activation(out=x1[:, blk, :], in_=r1[:, :], func=AF.Identity, bias=stt[:, 16:17], scale=stt[:, 15:16]),
                     reads=[br1, bst], writes=[B_x1[blk]])
                xi3 = blk % 4
                xb, bxb = (scr[3] if xi3 < 2 else scr[5]), B_xb[xi3]
                xb16 = xb[:].bitcast(BF16)[:, (xi3 % 2) * 1024:(xi3 % 2 + 1) * 1024]
                P.op("act", lambda e: e.activation(out=xb16, in_=r1[:, :], func=AF.Identity, bias=stt[:, 16:17], scale=stt[:, 15:16]),
                     reads=[br1, bst], writes=[bxb])
                c.update(xb16=xb16, bxb=bxb)

            def o_C(blk):
                if blk % 2 == 0:
                    return
                b0 = blk - 1
                cs = (octx[b0], octx[blk])
                bt0 = ps1()
                bt1 = ps1()
                XT0 = ps[:, bt0, :].bitcast(BF16).rearrange("p (e t) -> p e t", e=4)
                XT1 = ps[:, bt1, :].bitcast(BF16).rearrange("p (e t) -> p e t", e=4)

                def fx(e):
                    r = None
                    for bi in range(2):
                        xb16 = cs[bi]["xb16"]
                        for e_ in range(8):
                            dst = (XT0 if e_ % 2 == 0 else XT1)[:, e_ // 2, bi * 128:(bi + 1) * 128]
                            r = e.transpose(out=dst, in_=xb16[:, e_ * 128:(e_ + 1) * 128], identity=identb[:])
                    return r
                mm_group(fx, [cs[0]["bxb"], cs[1]["bxb"], B_c["identb"]], [B_ps[bt0], B_ps[bt1]])
                for e_ in range(8):
                    o_ap = x1T[:, e_, b0 * 128:(b0 + 2) * 128]
                    if e_ % 2 == 0:
                        P.op("act", (lambda e_, o_ap: lambda e: e.activation(out=o_ap, in_=XT0[:, e_ // 2, :], func=AF.Identity,
                                                                            bias=bcol[:, 60 + e_:61 + e_], scale=bcol[:, 52 + e_:53 + e_]))(e_, o_ap),
                             reads=[B_ps[bt0], B_c["bcol"]], writes=[B_x1T[e_]])
                    else:
                        P.op("dve", (lambda e_, o_ap: lambda e: e.tensor_scalar(out=o_ap, in0=XT1[:, e_ // 2, :], scalar1=bcol[:, 52 + e_:53 + e_],
                                                                               scalar2=bcol[:, 60 + e_:61 + e_], op0=ALU.mult, op1=ALU.add))(e_, o_ap),
                             reads=[B_ps[bt1], B_c["bcol"]], writes=[B_x1T[e_]])

            B_hT_lo = Buf("hT_lo")
            Prog.alias_after([B_hT_lo], [B_attnT])
            B_hT = [B_hT_lo, None]

            def up_unit(fh, fi, th, sq_half=None):
                fidx = fh * 16 + fi
                slot, bslot = acquire((g, "up", fidx // 4))
                wv = slot[:, :].rearrange("p (k n) -> p k n", k=8)
                b = ps1()

                def f(e):
                    r = None
                    for k in range(8):
                        r = e.matmul(ps[:, b, :], lhsT=wv[:, k, (fidx % 4) * 128:(fidx % 4 + 1) * 128],
                                     rhs=x1T[:, k, th * 512:(th + 1) * 512], start=(k == 0), stop=(k == 7))
                    return r
                mm_group(f, [bslot] + B_x1T, [B_ps[b]])
                if sq_half is None:
                    sq_t, bsq = nscr()
                    sq = sq_t[:, 0:512]
                else:
                    sq, bsq = scr[4][:, sq_half * 512:(sq_half + 1) * 512], B_scr[4]
                P.op("act", lambda e: e.activation(out=sq, in_=ps[:, b, :], func=AF.Square), reads=[B_ps[b]], writes=[bsq])
                P.op("dve", lambda e: e.scalar_tensor_tensor(out=hT[:, fi, th * 512:(th + 1) * 512], in0=ps[:, b, :],
                                                             scalar=0.0, in1=sq, op0=ALU.is_gt, op1=ALU.mult),
                     reads=[B_ps[b], bsq], writes=[B_hT[fi // 8]])


            def p6_hook(t):
                if t == 7:
                    B_hT_hi = Buf("hT_hi")
                    Prog.alias_after([B_hT_hi], [B_mixT])
                    B_hT[1] = B_hT_hi
                if 7 <= t <= 10:
                    for fi in (2 * (t - 7), 2 * (t - 7) + 1, 8 + 2 * (t - 7), 9 + 2 * (t - 7)):
                        up_unit(0, fi, 0, sq_half=fi % 2)

            B_xb = [Buf("xb%d" % i_) for i_ in range(4)]
            Prog.alias_after(B_xb[0:2], [B_scr[3]])
            Prog.alias_after(B_xb[2:4], [B_scr[5]])
            run_pipeline(NB, [o_A, o_B, o_C], [0, 1, 3], hook=p6_hook)
            Prog.alias_after([B_scr[3]], B_xb[0:2])
            Prog.alias_after([B_scr[5]], B_xb[2:4])
            release((g, "o", 0))
            release((g, "o", 1))

            chk(5)
            for fh in range(2):
                if fh == 0:
                    for fi in range(16):
                        up_unit(0, fi, 1)
                        if fi % 4 == 3:
                            release((g, "up", fi // 4))
                    rest = range(0)
                else:
                    rest = range(16)
                for fi in rest:
                    for th in range(2):
                        up_unit(fh, fi, th)
                    if fi % 4 == 3:
                        release((g, "up", (fh * 16 + fi) // 4))
                if fh == 1:
                    load_ln(2)
                dsl = [acquire((g, "dn", fh * 4 + i)) for i in range(4)]
                wd = [s[0][:, :].rearrange("p (f n) -> p f n", f=4) for s in dsl]
                dctx = [dict() for _ in range(NB)]

                def d_A(blk, fh=fh, wd=wd, dsl=dsl):
                    b2 = ps2()

                    def f(e):
                        r = None
                        for fi in range(16):
                            for half in range(2):
                                r = e.matmul(ps[:, b2 + half, :], lhsT=hT[:, fi, blk * 128:(blk + 1) * 128],
                                             rhs=wd[fi // 4][:, fi % 4, half * 512:(half + 1) * 512], start=(fi == 0), stop=(fi == 15))
                        return r
                    mm_group(f, [s_[1] for s_ in dsl] + B_hT, [B_ps[b2], B_ps[b2 + 1]])
                    pflat = ps[:, b2:b2 + 2, :].rearrange("p b n -> p (b n)")
                    if fh == 0:
                        P.op("dve", lambda e: e.tensor_tensor(out=x1[:, blk, :], in0=x1[:, blk, :], in1=lnbuf[:, 0, :], op=ALU.mult),
                             reads=[B_x1[blk], B_ln], writes=[B_x1[blk]])
                        P.op("dve", lambda e: e.tensor_tensor(out=x1[:, blk, :], in0=x1[:, blk, :], in1=lnbuf[:, 1, :], op=ALU.add),
                             reads=[B_x1[blk], B_ln], writes=[B_x1[blk]])
                        P.op("dve", lambda e: e.scalar_tensor_tensor(out=x1[:, blk, :], in0=x1[:, blk, :], scalar=ALPHA, in1=pflat,
                                                                     op0=ALU.mult, op1=ALU.add),
                             reads=[B_x1[blk], B_ps[b2], B_ps[b2 + 1]], writes=[B_x1[blk]])
                    else:
                        r2, br2 = scr[blk % 3], B_scr[blk % 3]
                        P.op("dve", lambda e: e.tensor_tensor(out=r2[:, :], in0=x1[:, blk, :], in1=pflat, op=ALU.add),
                             reads=[B_x1[blk], B_ps[b2], B_ps[b2 + 1]], writes=[br2])
                        stt, bst = ln_stats(r2[:, :], br2)
                        dctx[blk].update(r2=r2, br2=br2, stt=stt, bst=bst)

                def d_B(blk):
                    c = dctx[blk]
                    ln_normalize(c["r2"][:, :], c["br2"], c["stt"], c["bst"])

                def d_C(blk, g=g):
                    c = dctx[blk]
                    r2, br2 = c["r2"], c["br2"]
                    ln_affine(r2[:, :], br2, lnbuf[:, 0, :], lnbuf[:, 1, :], r2[:, :], [br2])
                    row0 = g * G + blk * 128
                    out_ops.append(P.op("sp", lambda e: e.dma_start(out=y_d[row0:row0 + 128, :], in_=r2[:, :]), reads=[br2], dma=True))

                if fh == 0:
                    for blk in range(NB):
                        d_A(blk)
                else:
                    run_pipeline(NB, [d_A, d_B, d_C], [0, 1, 2])
                for i in range(4):
                    release((g, "dn", fh * 4 + i))
            prev_B_x1 = B_x1
            prev_B_x1T = B_x1T
            prev_B_hT = B_hT

        except _Stop:
            dz, bdz = nscr()
            P.op("dve", lambda e: e.memset(dz[:, :], 0.0), writes=[bdz])
            out_ops.append(P.op("sp", lambda e: e.dma_start(out=y_d[0:128, :], in_=dz[:, :]), reads=[bdz], dma=True))
        P.build(final_waits=out_ops)
    return nc


def _rep(v, n=128):
    return np.ascontiguousarray(np.broadcast_to(np.asarray(v, np.float32)[None], (n,) + tuple(np.shape(v))))


STOP = None


def kernel(x, w_in, b_in, attn_sinks, gmlp_ln_g, gmlp_ln_b, gmlp_w_s, gmlp_b_s,
           w_branch_attn, w_branch_gmlp, w_out, ln1_g, ln1_b, w_up, w_down, ln2_g, ln2_b):
    f = lambda a: np.ascontiguousarray(np.asarray(a, dtype=np.float32))
    x = f(x)
    w_in0 = f(w_in)[0]
    b = f(b_in)[0]
    bq = b[0:1024].reshape(8, 128).T
    bk = b[1024:1280].reshape(4, 64)
    bkd = np.concatenate([b[1024:1280].reshape(2, 128).T, np.zeros((128, 2), np.float32)], axis=1)
    bzu = b[1536:2560].reshape(8, 128).T
    bga = b[3584:4608].reshape(8, 128).T
    bgg = b[4608:5632].reshape(8, 128).T
    gcol = f(gmlp_ln_g)[0].reshape(8, 128).T
    gbcol = f(gmlp_ln_b)[0].reshape(8, 128).T
    g1col = f(ln1_g)[0].reshape(8, 128).T
    b1col = f(ln1_b)[0].reshape(8, 128).T
    bcol = f(np.concatenate([bq, bkd, bzu, bga, bgg, gcol, gbcol, g1col, b1col], axis=1))
    brow = f(_rep(b[1280:1536]))
    bzv = f(b[2560:3584].reshape(1, 1024))
    lnrep = f(np.stack([_rep(f(gmlp_ln_g)[0]), _rep(f(gmlp_ln_b)[0]), _rep(f(ln1_g)[0]), _rep(f(ln1_b)[0]),
                        _rep(f(ln2_g)[0]), _rep(f(ln2_b)[0])], axis=1))
    sinks = _rep(f(attn_sinks)[0])
    wsT = f(np.transpose(f(gmlp_w_s)[0], (2, 0, 1)))
    si = np.arange(128)
    trilT = f((si[:, None] <= si[None, :]).astype(np.float32))
    bs = f(f(gmlp_b_s)[0].reshape(1, 1024))
    m_prev = (si[None, :] > si[:, None]).astype(np.float32)
    m_cur = (si[None, :] <= si[:, None]).astype(np.float32)
    NEG = -29952.0
    a1 = np.concatenate([m_prev, m_cur], axis=1)
    a0 = np.concatenate([np.zeros_like(m_prev), m_cur], axis=1)
    amask = f(np.tile((1.0 - a1) * NEG, (1, 2)))
    amask_first = f(np.tile((1.0 - a0) * NEG, (1, 2)))
    ident = f(np.eye(128, dtype=np.float32))
    shared = {
        "w_in": w_in0, "w_bra": f(w_branch_attn)[0], "w_brg": f(w_branch_gmlp)[0], "w_out": f(w_out)[0],
        "w_up": f(w_up)[0], "w_down": f(w_down)[0], "bcol": bcol, "brow": brow, "bzv": bzv, "lnrep": lnrep, "sinks": sinks,
        "wsT": wsT, "trilT": trilT, "bs": bs, "amask": amask, "ident": ident,
    }
    in_maps = []
    for c in range(NCORES):
        bi, half = c // 2, c % 2
        t0 = half * TOK
        xs = x[bi, t0:t0 + TOK]
        if half == 0:
            halo = np.zeros((128, D), np.float32)
            am0 = amask_first
        else:
            halo = x[bi, t0 - 128:t0]
            am0 = amask
        xT = f(np.concatenate([halo, xs], axis=0).T)
        m = dict(shared)
        m["xT"] = xT
        m["x"] = f(xs)
        m["amask0"] = am0
        in_maps.append(m)
    nc = build_program(STOP)
    res = run_bass_kernel_spmd(nc, in_maps, core_ids=list(range(NCORES)))
    out = np.empty((BATCH, SEQ, D), np.float32)
    for c in range(NCORES):
        bi, half = c // 2, c % 2
        out[bi, half * TOK:(half + 1) * TOK] = res.results[c]["y"]
    return out
```
